# Optimizing a Trainium2 kernel written in Bass

```python
import jax, jax.numpy as jnp
from jax import lax
import numpy as np

D_MODEL = 1024
BATCH = 2
SEQ = 16384
DEPTH = 1
DEC_BATCH = 32
DEC_SEQ = 32
PAST_LEN = 2048

CHUNK = 64
D_MIX = D_MODEL
W_A = D_MIX // 2
W_B = D_MIX - W_A
N_HEADS_A = 8
CONV_W = 3
HIST_C = CONV_W - 1
POOL_WINDOWS = (2, 4, 8, 16)
N_GROUPS_B = len(POOL_WINDOWS)
GC = W_B // N_GROUPS_B
HIST_P = max(POOL_WINDOWS) - 1
PLE_DIM = 256
PROJ_W = 4 * W_A + 2 * W_B
EPS = 1e-6

kernel_name = "hybrid_shortconv_pool_stream_step"


def rmsnorm(x, g):
    x32 = x.astype(jnp.float32)
    y = x32 * lax.rsqrt(jnp.mean(x32 * x32, axis=-1, keepdims=True) + EPS)
    return (y * g.astype(jnp.float32)).astype(x.dtype)


def short_conv_mixer(hA, bA, cA, zA, hist, conv_w, conv_b):
    T = hA.shape[1]
    u = cA * hA
    uu = jnp.concatenate([hist.astype(u.dtype), u], axis=1)
    c = (uu[:, 0:T] * conv_w[0] + uu[:, 1:1 + T] * conv_w[1]
         + uu[:, 2:2 + T] * conv_w[2] + conv_b)
    y = bA * c * jax.nn.silu(zA)
    return y, uu[:, -HIST_C:]


def pooling_mixer(vB, zB, hist, pos, pool_w, pool_scale):
    Bsz, T, _ = vB.shape
    vv = jnp.concatenate([hist.astype(vB.dtype), vB], axis=1).astype(jnp.float32)
    cs = jnp.concatenate([jnp.zeros((Bsz, 1, W_B), jnp.float32),
                          jnp.cumsum(vv, axis=1)], axis=1)
    v32 = vB.astype(jnp.float32)
    outs = []
    for g, w in enumerate(POOL_WINDOWS):
        sl = slice(g * GC, (g + 1) * GC)
        s = cs[:, HIST_P + 1:HIST_P + 1 + T, sl] - cs[:, HIST_P + 1 - w:HIST_P + 1 - w + T, sl]
        cnt = jnp.minimum(pos + 1, w).astype(jnp.float32)[None, :, None]
        outs.append(s / cnt - v32[:, :, sl])
    pooled = jnp.concatenate(outs, axis=-1).astype(vB.dtype)
    pg = pooled.reshape(Bsz, T, N_GROUPS_B, GC)
    mixed = jnp.einsum('btgc,gcd->btgd', pg, pool_w).reshape(Bsz, T, W_B) * pool_scale
    y = mixed * jax.nn.silu(zB)
    return y, vv[:, -HIST_P:].astype(vB.dtype)


def layer(h, p, conv_hist, pool_hist, pos, g_mix, w_in, conv_w, conv_b, pool_w, pool_scale,
          w_out, g_ple, w_ple_gate, w_ple):
    hn = rmsnorm(h, g_mix)
    proj = hn @ w_in
    hA, bA, cA, zA, vB, zB = jnp.split(
        proj, [W_A, 2 * W_A, 3 * W_A, 4 * W_A, 4 * W_A + W_B], axis=-1)
    yA, new_conv = short_conv_mixer(hA, bA, cA, zA, conv_hist, conv_w, conv_b)
    yB, new_pool = pooling_mixer(vB, zB, pool_hist, pos, pool_w, pool_scale)
    h = h + jnp.concatenate([yA, yB], axis=-1) @ w_out
    gate = jax.nn.sigmoid(rmsnorm(h, g_ple) @ w_ple_gate)
    h = h + (p @ w_ple) * gate
    return h, new_conv, new_pool


def run_trunk(x, p, conv_hist, pool_hist, pos, g_mix, w_in, conv_w, conv_b, pool_w, pool_scale,
              w_out, g_ple, w_ple_gate, w_ple, g_final):
    h = x
    convs, pools = [], []
    for i in range(DEPTH):
        h, nc, npl = layer(h, p[i], conv_hist[i], pool_hist[i], pos, g_mix[i], w_in[i],
                           conv_w[i], conv_b[i], pool_w[i], pool_scale[i], w_out[i],
                           g_ple[i], w_ple_gate[i], w_ple[i])
        convs.append(nc)
        pools.append(npl)
    return rmsnorm(h, g_final), jnp.stack(convs, 0), jnp.stack(pools, 0)


def setup_inputs(seed: int = 0) -> dict:
    key = jax.random.key(seed)
    ks = jax.random.split(key, 20)
    f32 = jnp.float32
    nrm = lambda k, shape, s: jax.random.normal(k, shape, f32) * s
    return {
        "x_prompt": nrm(ks[0], (BATCH, SEQ, D_MODEL), 1.0),
        "x_sample": nrm(ks[1], (DEC_BATCH, DEC_SEQ, D_MODEL), 1.0),
        "cache_conv": nrm(ks[2], (DEPTH, DEC_BATCH, HIST_C, W_A), 1.0),
        "cache_pool": nrm(ks[3], (DEPTH, DEC_BATCH, HIST_P, W_B), 1.0),
        "p_prompt": nrm(ks[4], (DEPTH, BATCH, SEQ, PLE_DIM), 1.0),
        "p_sample": nrm(ks[5], (DEPTH, DEC_BATCH, DEC_SEQ, PLE_DIM), 1.0),
        "g_mix": 1.0 + nrm(ks[6], (DEPTH, D_MODEL), 0.05),
        "w_in": nrm(ks[7], (DEPTH, D_MODEL, PROJ_W), D_MODEL ** -0.5),
        "conv_w": nrm(ks[8], (DEPTH, CONV_W, W_A), CONV_W ** -0.5),
        "conv_b": nrm(ks[9], (DEPTH, W_A), 0.02),
        "pool_w": nrm(ks[10], (DEPTH, N_GROUPS_B, GC, GC), GC ** -0.5),
        "pool_scale": 1.0 + nrm(ks[11], (DEPTH, W_B), 0.1),
        "w_out": nrm(ks[12], (DEPTH, D_MIX, D_MODEL), D_MIX ** -0.5),
        "g_ple": 1.0 + nrm(ks[13], (DEPTH, D_MODEL), 0.05),
        "w_ple_gate": nrm(ks[14], (DEPTH, D_MODEL, D_MODEL), D_MODEL ** -0.5),
        "w_ple": nrm(ks[15], (DEPTH, PLE_DIM, D_MODEL), PLE_DIM ** -0.5),
        "g_final": 1.0 + nrm(ks[16], (D_MODEL,), 0.05),
    }


def reference(x_prompt, x_sample, cache_conv, cache_pool, p_prompt, p_sample, g_mix, w_in,
              conv_w, conv_b, pool_w, pool_scale, w_out, g_ple, w_ple_gate, w_ple, g_final):
    conv0 = jnp.zeros((DEPTH, BATCH, HIST_C, W_A), x_prompt.dtype)
    pool0 = jnp.zeros((DEPTH, BATCH, HIST_P, W_B), x_prompt.dtype)
    pos_prompt = jnp.arange(SEQ, dtype=jnp.int32)
    y_prompt, state_conv_prompt, state_pool_prompt = run_trunk(
        x_prompt, p_prompt, conv0, pool0, pos_prompt, g_mix, w_in, conv_w, conv_b, pool_w,
        pool_scale, w_out, g_ple, w_ple_gate, w_ple, g_final)
    T = x_sample.shape[1]
    pos_sample = PAST_LEN + jnp.arange(T, dtype=jnp.int32)
    y_sample, state_conv_sample, state_pool_sample = run_trunk(
        x_sample, p_sample, cache_conv, cache_pool, pos_sample, g_mix, w_in, conv_w, conv_b,
        pool_w, pool_scale, w_out, g_ple, w_ple_gate, w_ple, g_final)
    return (y_prompt, y_sample, state_conv_prompt, state_pool_prompt, state_conv_sample, state_pool_sample)
```

```python
import numpy as np
from contextlib import ExitStack
import concourse.bass as bass
import concourse.mybir as mybir
from concourse.bass_utils import run_bass_kernel_spmd

F32 = mybir.dt.float32
BF16 = mybir.dt.bfloat16
AF = mybir.ActivationFunctionType
ALU = mybir.AluOpType

D = 1024
PW = 3072
PLE = 256
EPS = 1e-6
HALO = 16
NCORES = 8


class Sched:
    ENGS = ("pe", "act", "dve", "pool", "sp")

    def __init__(self):
        self.ops = {e: [] for e in self.ENGS}
        self.last_writer = {}
        self.readers = {}
        self.prev_access = {}
        self.dma_count = {}
        self.total_wait = set()

    def op(self, eng, fn, reads=(), writes=(), dma=None):
        idx = len(self.ops[eng])
        deps = {}

        def add(d, raw):
            if d is None:
                return
            deps[d] = deps.get(d, False) or raw

        for k in reads:
            if isinstance(k, str) and k.startswith("PS:"):
                pa = self.prev_access.get(k)
                if pa is not None:
                    add((pa[0], pa[1]), pa[2])
            else:
                add(self.last_writer.get(k), True)
        for k in writes:
            if isinstance(k, str) and k.startswith("PS:"):
                pa = self.prev_access.get(k)
                if pa is not None:
                    add((pa[0], pa[1]), False)
            else:
                add(self.last_writer.get(k), False)
                for r in self.readers.get(k, ()):
                    add(r, False)
        rec = dict(fn=fn, deps=deps, dma=dma, signal=False, dma_val=None)
        if dma is not None:
            self.dma_count[dma] = self.dma_count.get(dma, 0) + 16
            rec["dma_val"] = self.dma_count[dma]
        self.ops[eng].append(rec)
        for k in reads:
            if isinstance(k, str) and k.startswith("PS:"):
                self.prev_access[k] = (eng, idx, False)
            else:
                self.readers.setdefault(k, []).append((eng, idx))
        for k in writes:
            if isinstance(k, str) and k.startswith("PS:"):
                self.prev_access[k] = (eng, idx, True)
            else:
                self.last_writer[k] = (eng, idx)
                self.readers[k] = []
        return (eng, idx)

    def finalize(self):
        for eng in self.ENGS:
            for rec in self.ops[eng]:
                keep = []
                for (de, di), raw in rec["deps"].items():
                    drec = self.ops[de][di]
                    if de == eng and drec["dma"] is None:
                        if eng == "pe":
                            continue
                    keep.append((de, di))
                    if drec["dma"] is None:
                        drec["signal"] = True
                rec["keep"] = keep
        for eng in self.ENGS:
            c = 0
            for rec in self.ops[eng]:
                if rec["dma"] is None and rec["signal"]:
                    c += 1
                    rec["sig_val"] = c

    def emit(self, eng, engine, sems, dma_sems):
        known = {}
        for rec in self.ops[eng]:
            waits = {}
            for (de, di) in rec["keep"]:
                drec = self.ops[de][di]
                if drec["dma"] is not None:
                    key = ("dma", drec["dma"])
                    val = drec["dma_val"]
                    if drec["dma"] in self.total_wait:
                        val = self.dma_count[drec["dma"]]
                else:
                    key = ("eng", de)
                    val = drec["sig_val"]
                if val > waits.get(key, 0):
                    waits[key] = val
            for key, val in waits.items():
                if known.get(key, 0) >= val:
                    continue
                known[key] = val
                sem = dma_sems[key[1]] if key[0] == "dma" else sems[key[1]]
                engine.wait_ge(sem, val)
            ins = rec["fn"](engine)
            if rec["dma"] is not None:
                ins.then_inc(dma_sems[rec["dma"]], 16)
            elif rec["signal"]:
                ins.then_inc(sems[eng], 1)


def build(NPS):
    NT = NPS + 1
    NPTOK = NPS * 512
    nc = bass.Bass("TRN2", target_bir_lowering=False)

    def din(name, shape):
        return nc.dram_tensor(name, shape, F32, kind="ExternalInput").ap()

    def dout(name, shape):
        return nc.dram_tensor(name, shape, F32, kind="ExternalOutput").ap()

    xp = din("xp", [HALO + NPTOK, D])
    pp = din("pp", [NPTOK, PLE])
    xs = din("xs", [128, D])
    ps_in = din("ps", [128, PLE])
    cconvT = din("cconvT", [128, 4, 4, 2])
    cpoolT = din("cpoolT", [128, 4, 4, 15])
    w_in = din("w_in", [24, 128, D])
    w_out = din("w_out", [D, D])
    w_gate = din("w_gate", [D, D])
    w_ple = din("w_ple", [PLE, D])
    pool_w = din("pool_w", [4, 128, 128])
    prm_in = din("prm", [128, 36])
    gfin_in = din("gfin", [128, D])
    icnt_in = din("icnt", [128, 4, 16])
    ident_in = nc.dram_tensor("ident", [128, 128], BF16, kind="ExternalInput").ap()

    yp = dout("yp", [NPTOK, D])
    ys = dout("ys", [128, D])
    sconv_p = dout("sconv_p", [128, 4, 2])
    spool_p = dout("spool_p", [128, 4, 15])
    sconv_s = dout("sconv_s", [128, 4, 4, 2])
    spool_s = dout("spool_s", [128, 4, 4, 15])

    S = Sched()
    es = ExitStack()
    with es:
        def sb(name, shape, dt):
            return es.enter_context(nc.sbuf_tensor(name, shape, dt))

        def psum(name, shape, dt):
            return es.enter_context(nc.psum_tensor(name, shape, dt))

        NXB = 10
        NPB = 1
        xh = sb("xh", [128, NXB, D], F32)
        pin = sb("pin", [128, NPB, PLE], F32)
        winb = sb("winb", [128, 8, PW], BF16)
        woutb = sb("woutb", [128, 8, D], BF16)
        wgb = sb("wgb", [128, 8, D], BF16)
        wpb = sb("wpb", [128, 2, D], BF16)
        pwb = sb("pwb", [128, 4, 128], BF16)
        prm = sb("prm_sb", [128, 36], F32)
        gfin = sb("gfin_sb", [128, D], F32)
        icnt = sb("icnt_sb", [128, 4, 16], F32)
        idb = sb("idb", [128, 128], BF16)
        nh = sb("nh", [128, 1], F32)
        NHN = 3
        NPT = 3
        hn = [sb(f"hn{i}", [128, D], BF16) for i in range(NHN)]
        pb = [sb(f"pb{i}", [128, PLE], BF16) for i in range(1)]
        hnT = [sb(f"hnT{i}", [128, 8, 512], BF16) for i in range(2)]
        hnTh = sb("hnTh", [128, 8, HALO], BF16)
        yT = [sb(f"yT{i}", [128, 8, 512], BF16) for i in range(2)]
        hn2T = [sb(f"hn2T{i}", [128, 8, 128], BF16) for i in range(2)]
        pT = [sb(f"pT{i}", [128, 2, 128], BF16) for i in range(NPT)]
        th = [sb(f"th{i}", [128, D], F32) for i in range(1)]
        xhalo = th[0][0:HALO, :]
        UP = [sb(f"UP{c}", [128, 1, 2 + 512], F32) for c in range(4)]
        VP = [sb(f"VP{g}", [128, 1, 16 + 512], F32) for g in range(4)]
        US = [UP[c][:, 0, 0:4 * 34].rearrange("p (s t) -> p s t", s=4) for c in range(4)]
        VS = [VP[g][:, 0, 0:4 * 48].rearrange("p (s t) -> p s t", s=4) for g in range(4)]
        NPAR = 1
        tA2 = [sb(f"tA2_{i}", [128, 512], F32) for i in range(NPAR)]
        tA3 = [sb(f"tA3_{i}", [128, 512], F32) for i in range(NPAR)]
        tB1 = [sb(f"tB1_{i}", [128, 512], F32) for i in range(NPAR)]
        sA = [sb(f"sA_{i}", [128, 16 + 512], F32) for i in range(NPAR)]
        sB = [sb(f"sB_{i}", [128, 16 + 512], F32) for i in range(NPAR)]
        tB3 = [sb(f"tB3_{i}", [128, 512], BF16) for i in range(NPAR)]
        t16 = sb("t16", [128, 16], F32)
        t16h = sb("t16h", [128, 16], F32)
        icntp = sb("icntp", [128, 16], F32)
        NST = 3 * (NT * 4 + 2) + 8
        st = sb("stat", [128, NST], F32)

        TR = psum("TR", [128, 1024], BF16)
        NF, NTB = 4, 3
        FB = [psum(f"FB{i}", [128, 512], F32) for i in range(NF)]
        TB = [psum(f"TB{i}", [128, 512], F32) for i in range(NTB)]

        sems = {e: es.enter_context(nc.semaphore("s_" + e)) for e in Sched.ENGS}
        dma_names = ([f"ldx{i}" for i in range(NXB)] + [f"ldp{i}" for i in range(NPB)]
                     + [f"sty{i}" for i in range(NXB)] + [f"stg{i}" for i in range(4)]
                     + ["misc", "halo", "statep", "states", "cache", "gfin", "wo", "wp0", "wp1", "pw"])
        dma_sems = {n: es.enter_context(nc.semaphore("d_" + n)) for n in dma_names}
        S.total_wait.add("misc")
        S.total_wait.add("cache")
        S.total_wait.add("wo")
        S.total_wait.add("statep")
        S.total_wait.add("states")

        ctr = dict(stat=0, f=0, t=0, hn=0, par=0)

        def stat_alloc(n):
            c = ctr["stat"]
            ctr["stat"] += n
            assert ctr["stat"] <= NST
            return c

        def falloc():
            b = ctr["f"] % NF
            ctr["f"] += 1
            return b

        def talloc():
            b = ctr["t"] % NTB
            ctr["t"] += 1
            return b

        class Ring:
            def __init__(self, n, name):
                self.free = list(range(n))
                self.name = name

            def alloc(self):
                assert self.free, f"ring {self.name} exhausted"
                return self.free.pop(0)

            def release(self, s):
                assert s not in self.free
                self.free.append(s)

        hn_ring = Ring(NHN, "hn")
        pT_ring = Ring(NPT, "pT")
        hn2T_ring = Ring(2, "hn2T")

        def hnalloc():
            return hn_ring.alloc()

        S.op("sp", lambda e: e.dma_start(out=prm[:], in_=prm_in), writes=["prm"], dma="misc")
        S.op("sp", lambda e: e.dma_start(out=idb[:], in_=ident_in), writes=["idb"], dma="misc")
        S.op("sp", lambda e: e.dma_start(out=xhalo, in_=xp[0:HALO, :]), writes=[("th", 0, 0), ("th", 0, 1)], dma="halo")

        slot_of = {}
        ldstate = dict(next=0, stores=0)

        def xslot(i, j):
            return slot_of[(i, j)]

        def nsub(i):
            return 4 if i < NPS else 1

        olist = [(i, j) for i in range(NT) for j in range(nsub(i))]
        oindex = {ij: q for q, ij in enumerate(olist)}

        def emit_load(q):
            i, j = olist[q]
            slot = q % NXB
            slot_of[(i, j)] = slot
            if i < NPS:
                r0 = (i * 4 + j) * 128
                xsrc = xp[HALO + r0:HALO + r0 + 128, :]
            else:
                xsrc = xs
            S.op("sp", lambda e: e.dma_start(out=xh[:, slot, :], in_=xsrc), writes=[("xh", slot)], dma=f"ldx{slot}")

        def emit_pload(q):
            if q >= len(olist):
                return
            i, j = olist[q]
            ps_ = q % NPB
            if i < NPS:
                r0 = (i * 4 + j) * 128
                psrc = pp[r0:r0 + 128, :]
            else:
                psrc = ps_in
            S.op("sp", lambda e: e.dma_start(out=pin[:, ps_, :], in_=psrc), writes=[("pin", ps_)], dma=f"ldp{ps_}")

        def try_loads(limit=None):
            while ldstate["next"] < len(olist) and ldstate["next"] < ldstate["stores"] + NXB:
                if limit is not None and ldstate["next"] >= limit:
                    break
                emit_load(ldstate["next"])
                ldstate["next"] += 1

        try_loads(limit=4)
        S.op("sp", lambda e: e.dma_start(out=icnt[:], in_=icnt_in), writes=["icnt"], dma="misc")
        emit_pload(0)

        def emit_cache_loads():
            for c in range(4):
                S.op("sp", lambda e, c=c: e.dma_start(out=US[c][:, :, 0:2], in_=cconvT[:, c, :, :]),
                     writes=[("U", c)], dma="cache")
                S.op("sp", lambda e, c=c: e.dma_start(out=VS[c][:, :, 1:16], in_=cpoolT[:, c, :, :]),
                     writes=[("V", c)], dma="cache")
        S.op("pool", lambda e: e.memset(nh[:], -0.5), writes=["nh"])

        yT1f = yT[1][:].bitcast(F32).rearrange("p k n -> p (k n)")
        stg_views = [xh[:, NXB - 1, :], th[0][:, :], yT1f[:, 0:1024], yT1f[:, 1024:2048]]
        stg_keys = [[("xh", NXB - 1)], [("th", 0, 0), ("th", 0, 1)],
                    [(("yT", 1), c) for c in range(4)], [(("yT", 1), c) for c in range(4, 8)]]
        stg_ctr = [0]

        def stage(src_ap, shape3, consume):
            i = stg_ctr[0] % 4
            stg_ctr[0] += 1
            v = stg_views[i]
            if shape3 is not None:
                ncol = shape3[0] * shape3[1]
                v = v[:, 0:ncol].rearrange("p (a n) -> p a n", a=shape3[0])
            S.op("sp", lambda e: e.dma_start(out=v, in_=src_ap), writes=stg_keys[i], dma=f"stg{i}")
            consume(v, stg_keys[i])

        chunk_groups = []
        for c in range(4):
            chunk_groups.append([16 + c, 20 + c])
            chunk_groups.append([c, 8 + c, 12 + c, 4 + c])
        gmix_b = prm[:, 0:8].unsqueeze(2).to_broadcast([128, 8, 128])
        win_ctr = [0]

        def emit_win_group(gidx):
            if gidx >= len(chunk_groups):
                return
            for cc in chunk_groups[gidx]:
                eng = "dve"
                win_ctr[0] += 1

                def consume(v, keys, cc=cc, eng=eng):
                    S.op(eng, lambda e: e.tensor_tensor(out=winb[:, :, cc * 128:(cc + 1) * 128], in0=v, in1=gmix_b,
                                                        op=ALU.mult),
                         reads=keys + ["prm"], writes=[("winb", cc)])
                stage(w_in[cc].rearrange("p (k n) -> p k n", k=8), (8, 128), consume)

        def winb_key(k, c):
            return ("winb", c)

        def emit_wout_stage(k):
            S.op("pool", lambda e: e.dma_start(out=woutb[:, k, :], in_=w_out[k * 128:(k + 1) * 128, :]),
                 writes=[("woutb", k)], dma="wo")

        def emit_wgate_stage(k):
            def consume(v, keys):
                S.op("act", lambda e: e.activation(out=wgb[:, k, :], in_=v, func=AF.Copy, scale=prm[:, 8 + k:9 + k]),
                     reads=keys + ["prm"], writes=[("wgb", k)])
            stage(w_gate[k * 128:(k + 1) * 128, :], None, consume)

        def emit_wple_stage(k):
            S.op("pool", lambda e: e.dma_start(out=wpb[:, k, :], in_=w_ple[k * 128:(k + 1) * 128, :]),
                 writes=[("wpb", k)], dma=f"wp{k}")
            S.op("pool", lambda e: e.tensor_scalar(out=wpb[:, k, :], in0=wpb[:, k, :], scalar1=0.5, scalar2=0.0,
                                                   op0=ALU.mult, op1=ALU.add),
                 reads=[("wpb", k)], writes=[("wpb", k)])

        def emit_poolw_stage():
            S.op("pool", lambda e: e.dma_start(out=pwb[:], in_=pool_w.rearrange("g c d -> c g d")),
                 writes=["pwb"], dma="pw")

        chains = []

        def add_chain(stages):
            ch = list(stages)
            ch.pop(0)()
            if ch:
                chains.append(ch)
            return ch

        def tick():
            for ch in list(chains):
                ch.pop(0)()
                if not ch:
                    chains.remove(ch)

        def finish(ch):
            while ch:
                ch.pop(0)()
            if ch in chains:
                chains.remove(ch)

        def emit_norm(x_ap, ntok, xkeys):
            c = stat_alloc(1)
            h = hnalloc()
            ss = st[0:ntok, c:c + 1]
            ch = add_chain([
                lambda: S.op("act", lambda e: e.activation(out=hn[h][0:ntok, :], in_=x_ap, func=AF.Square, accum_out=ss),
                             reads=xkeys, writes=[("st", c), ("hn", h)]),
                lambda: (S.op("dve", lambda e: e.tensor_scalar(out=ss, in0=ss, scalar1=1.0 / D, scalar2=EPS, op0=ALU.mult,
                                                               op1=ALU.add), reads=[("st", c)], writes=[("st", c)]),
                         S.op("pool", lambda e: e.tensor_tensor(out=ss, in0=ss, in1=nh[0:ntok, :], op=ALU.pow),
                              reads=[("st", c), "nh"], writes=[("st", c)])),
                lambda: S.op("act", lambda e: e.activation(out=hn[h][0:ntok, :], in_=x_ap, func=AF.Copy, scale=ss),
                             reads=xkeys + [("st", c)], writes=[("hn", h)]),
            ])
            return h, ch

        def emit_transposes(src_tile, nch, ntok, src_keys, dst_ap, dst_keys):
            for k in range(nch):
                S.op("pe", lambda e, k=k: e.transpose(out=TR[:, k * 128:k * 128 + ntok],
                                                     in_=src_tile[0:ntok, k * 128:(k + 1) * 128],
                                                     identity=idb[0:ntok, 0:ntok]),
                     reads=src_keys + ["idb"], writes=["PS:TR"])
            src = TR[:, 0:nch * 128].rearrange("p (k t) -> p k t", k=nch)[:, :, 0:ntok]
            S.op("dve", lambda e: e.tensor_copy(out=dst_ap, in_=src), reads=["PS:TR"], writes=dst_keys)

        def proj_chunk(c, src_ap, src_keys, n):
            b = falloc()
            for k in range(8):
                S.op("pe", lambda e, k=k: e.matmul(FB[b][:, 0:n], lhsT=winb[:, k, c * 128:(c + 1) * 128],
                                                   rhs=src_ap[:, k, :], start=(k == 0), stop=(k == 7)),
                     reads=src_keys + [winb_key(k, c)], writes=[f"PS:F{b}"])
            tick()
            return b

        def cw(jj, c):
            col = 16 + jj * 4 + c
            return prm[:, col:col + 1]

        def cbias(c):
            return prm[:, 28 + c:29 + c]

        def pscale(g):
            return prm[:, 32 + g:33 + g]

        def mixer_A(c, tile):
            S_, T_, n = tile["S"], tile["T"], tile["n"]
            U = tile["U"][c]
            ukey = ("U", c)
            src, skeys = tile["hnT"], tile["hnT_keys"]
            par = 0

            def v3(ap):
                return ap.rearrange("p (s t) -> p s t", s=S_)

            bh = proj_chunk(c, src, skeys, n)
            S.op("act", lambda e: e.activation(out=U[:, :, 2:2 + T_], in_=v3(FB[bh][:, 0:n]), func=AF.Copy),
                 reads=[f"PS:F{bh}"], writes=[ukey])
            bc = proj_chunk(8 + c, src, skeys, n)
            S.op("dve", lambda e: e.tensor_tensor(out=U[:, :, 2:2 + T_], in0=v3(FB[bc][:, 0:n]),
                                                  in1=U[:, :, 2:2 + T_], op=ALU.mult),
                 reads=[f"PS:F{bc}", ukey], writes=[ukey])
            t3 = v3(tA3[par][:, 0:n])
            S.op("act", lambda e: e.activation(out=t3, in_=U[:, :, 0:T_], func=AF.Identity, scale=cw(0, c), bias=cbias(c)),
                 reads=[ukey, "prm"], writes=[("tA3", par)])
            bz = proj_chunk(12 + c, src, skeys, n)
            S.op("act", lambda e: e.activation(out=tA2[par][:, 0:n], in_=FB[bz][:, 0:n], func=AF.Silu),
                 reads=[f"PS:F{bz}"], writes=[("tA2", par)])
            S.op("dve", lambda e: e.scalar_tensor_tensor(out=t3, in0=U[:, :, 1:1 + T_], scalar=cw(1, c), in1=t3,
                                                         op0=ALU.mult, op1=ALU.add),
                 reads=[ukey, ("tA3", par), "prm"], writes=[("tA3", par)])
            S.op("dve", lambda e: e.scalar_tensor_tensor(out=t3, in0=U[:, :, 2:2 + T_], scalar=cw(2, c), in1=t3,
                                                         op0=ALU.mult, op1=ALU.add),
                 reads=[ukey, ("tA3", par), "prm"], writes=[("tA3", par)])
            bb = proj_chunk(4 + c, src, skeys, n)
            S.op("dve", lambda e: e.tensor_tensor(out=tA3[par][:, 0:n], in0=FB[bb][:, 0:n], in1=tA3[par][:, 0:n],
                                                  op=ALU.mult),
                 reads=[f"PS:F{bb}", ("tA3", par)], writes=[("tA3", par)])
            S.op("pool", lambda e: e.tensor_tensor(out=tile["yT"][:, c, :], in0=tA3[par][:, 0:n], in1=tA2[par][:, 0:n],
                                                   op=ALU.mult),
                 reads=[("tA3", par), ("tA2", par)], writes=[(tile["yT_key"], c)])
            if tile["kind"] == "p":
                if tile["last"]:
                    S.op("sp", lambda e: e.dma_start(out=sconv_p[:, c, :], in_=U[:, 0, T_:T_ + 2]),
                         reads=[ukey], dma="statep")
                else:
                    S.op("pool", lambda e: e.tensor_copy(out=U[:, :, 0:2], in_=U[:, :, T_:T_ + 2]),
                         reads=[ukey], writes=[ukey])
            else:
                S.op("sp", lambda e: e.dma_start(out=sconv_s[:, c, :, :], in_=U[:, :, T_:T_ + 2]),
                     reads=[ukey], dma="states")

        def mixer_B_front(g, tile):
            S_, T_, n = tile["S"], tile["T"], tile["n"]
            V = tile["V"][g]
            vkey = ("V", g)
            src, skeys = tile["hnT"], tile["hnT_keys"]
            par = 0
            W = 2 ** (g + 1)

            def v3(ap):
                return ap.rearrange("p (s t) -> p s t", s=S_)

            def sv(buf):
                return buf[:, 0:S_ * (16 + T_)].rearrange("p (s t) -> p s t", s=S_)

            bv = proj_chunk(16 + g, src, skeys, n)
            S.op("act", lambda e: e.activation(out=V[:, :, 16:16 + T_], in_=v3(FB[bv][:, 0:n]), func=AF.Copy),
                 reads=[f"PS:F{bv}"], writes=[vkey])
            bz = proj_chunk(20 + g, src, skeys, n)
            S.op("act", lambda e: e.activation(out=tB1[par][:, 0:n], in_=FB[bz][:, 0:n], func=AF.Silu),
                 reads=[f"PS:F{bz}"], writes=[("tB1", par)])
            a3, b3 = sv(sA[par]), sv(sB[par])
            aeng = "dve" if g < 3 else "pool"
            stages = []
            lo = -(W - 2)
            stages.append(lambda lo=lo: S.op(aeng, lambda e: e.tensor_tensor(
                out=a3[:, :, 16 + lo:16 + T_], in0=V[:, :, 16 + lo:16 + T_], in1=V[:, :, 15 + lo:15 + T_], op=ALU.add),
                reads=[vkey], writes=[("sA", par)]))
            fin, fkey = a3, ("sA", par)
            if W >= 4:
                lo = -(W - 4)
                stages.append(lambda lo=lo: S.op(aeng, lambda e: e.tensor_tensor(
                    out=b3[:, :, 16 + lo:16 + T_], in0=a3[:, :, 16 + lo:16 + T_], in1=a3[:, :, 14 + lo:14 + T_], op=ALU.add),
                    reads=[("sA", par)], writes=[("sB", par)]))
                fin, fkey = b3, ("sB", par)
            if W >= 8:
                lo = -(W - 8)
                stages.append(lambda lo=lo: S.op(aeng, lambda e: e.tensor_tensor(
                    out=a3[:, :, 16 + lo:16 + T_], in0=b3[:, :, 16 + lo:16 + T_], in1=b3[:, :, 12 + lo:12 + T_], op=ALU.add),
                    reads=[("sB", par)], writes=[("sA", par)]))
                fin, fkey = a3, ("sA", par)
            if W >= 16:
                stages.append(lambda: S.op(aeng, lambda e: e.tensor_tensor(
                    out=b3[:, :, 16:16 + T_], in0=a3[:, :, 16:16 + T_], in1=a3[:, :, 8:8 + T_], op=ALU.add),
                    reads=[("sA", par)], writes=[("sB", par)]))
                fin, fkey = b3, ("sB", par)

            def s_pooled(fin=fin, fkey=fkey):
                if aeng == "dve":
                    S.op("dve", lambda e: e.scalar_tensor_tensor(out=v3(tB3[par][:, 0:n]), in0=fin[:, :, 16:16 + T_],
                                                                 scalar=1.0 / W, in1=V[:, :, 16:16 + T_],
                                                                 op0=ALU.mult, op1=ALU.subtract),
                         reads=[fkey, vkey], writes=[("tB3", par)])
                else:
                    S.op("pool", lambda e: e.tensor_scalar(out=fin[:, :, 16:16 + T_], in0=fin[:, :, 16:16 + T_],
                                                           scalar1=1.0 / W, scalar2=0.0, op0=ALU.mult, op1=ALU.add),
                         reads=[fkey], writes=[fkey])
                    S.op("pool", lambda e: e.tensor_tensor(out=v3(tB3[par][:, 0:n]), in0=fin[:, :, 16:16 + T_],
                                                           in1=V[:, :, 16:16 + T_], op=ALU.subtract),
                         reads=[fkey, vkey], writes=[("tB3", par)])
                if tile["first"]:
                    S.op(aeng, lambda e: e.tensor_tensor(out=t16[:], in0=fin[:, 0, 16:32],
                                                         in1=(icnt[:, g, :] if aeng == "dve" else icntp[:, :]), op=ALU.mult),
                         reads=[fkey, "icnt"], writes=["t16"])
                    S.op(aeng, lambda e: e.tensor_tensor(out=tB3[par][:, 0:16], in0=t16[:], in1=V[:, 0, 16:32],
                                                         op=ALU.subtract),
                         reads=["t16", vkey], writes=[("tB3", par)])
            adds = stages
            stages = []
            for a_ in range(0, len(adds), 2):
                grp = adds[a_:a_ + 2]
                stages.append(lambda grp=grp: [f_() for f_ in grp])
            stages.append(s_pooled)
            def s_hist():
                if tile["kind"] == "p":
                    if tile["last"]:
                        S.op("sp", lambda e: e.dma_start(out=spool_p[:, g, :], in_=V[:, 0, T_ + 1:T_ + 16]),
                             reads=[vkey], dma="statep")
                    else:
                        S.op("pool", lambda e: e.tensor_copy(out=V[:, :, 0:16], in_=V[:, :, T_:T_ + 16]),
                             reads=[vkey], writes=[vkey])
                else:
                    S.op("sp", lambda e: e.dma_start(out=spool_s[:, g, :, :], in_=V[:, :, T_ + 1:T_ + 16]),
                         reads=[vkey], dma="states")
            stages.append(s_hist)
            bch = add_chain(stages)
            return par, bch

        def mixer_B_back(g, tile, par_ch):
            par, bch = par_ch
            finish(bch)
            n = tile["n"]
            bm = falloc()
            S.op("pe", lambda e: e.matmul(FB[bm][:, 0:n], lhsT=pwb[:, g, :], rhs=tB3[par][:, 0:n], start=True, stop=True),
                 reads=[("tB3", par), "pwb"], writes=[f"PS:F{bm}"])
            tick()
            S.op("dve", lambda e: e.scalar_tensor_tensor(out=tile["yT"][:, 4 + g, :], in0=FB[bm][:, 0:n], scalar=pscale(g),
                                                         in1=tB1[par][:, 0:n], op0=ALU.mult, op1=ALU.mult),
                 reads=[f"PS:F{bm}", ("tB1", par), "prm"], writes=[(tile["yT_key"], 4 + g)])

        def emit_halo_front():
            h, ch = emit_norm(xhalo, HALO, [("th", 0, 0), ("th", 0, 1)])
            finish(ch)
            emit_transposes(hn[h], 8, HALO, [("hn", h)], hnTh[:], ["hnTh"])
            hn_ring.release(h)

        def emit_halo_A(c):
            bh = proj_chunk(c, hnTh, ["hnTh"], HALO)
            S.op("act", lambda e: e.activation(out=t16h[:, 0:HALO], in_=FB[bh][:, 0:HALO], func=AF.Copy),
                 reads=[f"PS:F{bh}"], writes=["t16h"])
            bc = proj_chunk(8 + c, hnTh, ["hnTh"], HALO)
            S.op("dve", lambda e: e.tensor_tensor(out=UP[c][:, 0, 0:2], in0=FB[bc][:, HALO - 2:HALO],
                                                  in1=t16h[:, HALO - 2:HALO], op=ALU.mult),
                 reads=[f"PS:F{bc}", "t16h"], writes=[("U", c)])

        def emit_halo_B(g):
            bv = proj_chunk(16 + g, hnTh, ["hnTh"], HALO)
            S.op("act", lambda e: e.activation(out=VP[g][:, 0, 0:16], in_=FB[bv][:, 0:HALO], func=AF.Copy),
                 reads=[f"PS:F{bv}"], writes=[("V", g)])

        def tile_desc(i):
            par = i % 2
            if i < NPS:
                return dict(kind="p", S=1, T=512, n=512, U=UP, V=VP, hnT=hnT[par][:, :, :], hnT_keys=[("hnT", par)],
                            yT=yT[par][:, :, :], yT_key=("yT", par), first=(i == 0), last=(i == NPS - 1), idx=i)
            return dict(kind="s", S=4, T=32, n=128, U=US, V=VS, hnT=hnT[par][:, :, 0:128], hnT_keys=[("hnT", par)],
                        yT=yT[par][:, :, 0:128], yT_key=("yT", par), first=False, last=False, idx=i)

        nstate = {}

        def emit_N_norm(i, j):
            while (i, j) not in slot_of and chains:
                tick()
            slot = xslot(i, j)
            nstate[(i, j)] = emit_norm(xh[:, slot, :], 128, [("xh", slot)])

        def emit_N_tr(i, j):
            par = i % 2
            h, ch = nstate[(i, j)]
            finish(ch)
            emit_transposes(hn[h], 8, 128, [("hn", h)], hnT[par][:, :, j * 128:(j + 1) * 128], [("hnT", par)])
            hn_ring.release(h)

        def emit_N(i, j):
            emit_N_norm(i, j)
            emit_N_tr(i, j)

        ostate = {}

        def emit_pcast(q):
            if q >= len(olist):
                return
            S.op("act", lambda e: e.activation(out=pb[0][:], in_=pin[:, q % NPB, :], func=AF.Copy),
                 reads=[("pin", q % NPB)], writes=[("pb", 0)])

        def emit_O_wout(i, j):
            slot = xslot(i, j)
            par = i % 2
            b0, b1 = talloc(), talloc()
            ykeys = [(("yT", par), c) for c in range(8)]
            for k in range(8):
                for hf, b in ((0, b0), (1, b1)):
                    S.op("pe", lambda e, k=k, hf=hf, b=b: e.matmul(TB[b][:, :], lhsT=yT[par][:, k, j * 128:(j + 1) * 128],
                                                                   rhs=woutb[:, k, hf * 512:(hf + 1) * 512],
                                                                   start=(k == 0), stop=(k == 7)),
                         reads=ykeys + [("woutb", k)], writes=[f"PS:T{b}"])
            tick()
            q_o = oindex[(i, j)]
            pslot = pT_ring.alloc()
            pbslot = 0
            emit_pcast(q_o)
            emit_transposes(pb[pbslot], 2, 128, [("pb", pbslot)], pT[pslot][:], [("pT", pslot)])
            emit_pload(q_o + 1)
            for hf, b in ((0, b0), (1, b1)):
                S.op("dve", lambda e, hf=hf, b=b: e.tensor_tensor(out=xh[:, slot, hf * 512:(hf + 1) * 512], in0=TB[b][:, :],
                                                                  in1=xh[:, slot, hf * 512:(hf + 1) * 512], op=ALU.add),
                     reads=[f"PS:T{b}", ("xh", slot)], writes=[("xh", slot)])
            h, ch = emit_norm(xh[:, slot, :], 128, [("xh", slot)])
            ostate[(i, j)] = dict(h=h, pslot=pslot, ch=ch)

        def emit_O_tr2(i, j):
            o = ostate[(i, j)]
            finish(o["ch"])
            q = hn2T_ring.alloc()
            emit_transposes(hn[o["h"]], 8, 128, [("hn", o["h"])], hn2T[q][:], [("hn2T", q)])
            hn_ring.release(o["h"])
            o["q"] = q

        def emit_O_gate(i, j):
            o = ostate[(i, j)]
            slot = xslot(i, j)
            q, pslot = o["q"], o["pslot"]
            g0, g1 = talloc(), talloc()
            for k in range(8):
                for hf, b in ((0, g0), (1, g1)):
                    S.op("pe", lambda e, k=k, hf=hf, b=b: e.matmul(TB[b][:, :], lhsT=hn2T[q][:, k, :],
                                                                   rhs=wgb[:, k, hf * 512:(hf + 1) * 512],
                                                                   start=(k == 0), stop=(k == 7)),
                         reads=[("hn2T", q), ("wgb", k)], writes=[f"PS:T{b}"])
            tick()
            tq = 0
            for hf, b in ((0, g0), (1, g1)):
                S.op("act", lambda e, hf=hf, b=b: e.activation(out=th[tq][:, hf * 512:(hf + 1) * 512], in_=TB[b][:, :],
                                                               func=AF.Tanh, scale=0.5),
                     reads=[f"PS:T{b}"], writes=[("th", tq, hf)])
            p0, p1 = talloc(), talloc()
            for k in range(2):
                for hf, b in ((0, p0), (1, p1)):
                    S.op("pe", lambda e, k=k, hf=hf, b=b: e.matmul(TB[b][:, :], lhsT=pT[pslot][:, k, :],
                                                                   rhs=wpb[:, k, hf * 512:(hf + 1) * 512],
                                                                   start=(k == 0), stop=(k == 1)),
                         reads=[("pT", pslot), ("wpb", k)], writes=[f"PS:T{b}"])
            tick()
            for hf, b in ((0, p0), (1, p1)):
                S.op("dve", lambda e, hf=hf, b=b: e.scalar_tensor_tensor(out=th[tq][:, hf * 512:(hf + 1) * 512],
                                                                         in0=th[tq][:, hf * 512:(hf + 1) * 512], scalar=1.0,
                                                                         in1=TB[b][:, :], op0=ALU.add, op1=ALU.mult),
                     reads=[f"PS:T{b}", ("th", tq, hf)], writes=[("th", tq, hf)])
            hn2T_ring.release(q)
            pT_ring.release(pslot)

        def emit_O_fin(i, j):
            slot = xslot(i, j)
            x_ap = xh[:, slot, :]
            xkeys = [("xh", slot)]
            c = stat_alloc(1)
            ss = st[:, c:c + 1]
            if i < NPS:
                r0 = (i * 4 + j) * 128
                dst = yp[r0:r0 + 128, :]
            else:
                dst = ys

            def s00():
                S.op("pool", lambda e: e.tensor_tensor(out=x_ap, in0=x_ap, in1=th[0][:], op=ALU.add),
                     reads=xkeys + [("th", 0, 0), ("th", 0, 1)], writes=xkeys)

            def s0():
                S.op("act", lambda e: e.activation(out=th[0][:, :], in_=x_ap, func=AF.Square, accum_out=ss),
                     reads=xkeys, writes=[("st", c), ("th", 0, 0), ("th", 0, 1)])

            def s1():
                S.op("dve", lambda e: e.tensor_scalar(out=ss, in0=ss, scalar1=1.0 / D, scalar2=EPS, op0=ALU.mult,
                                                      op1=ALU.add), reads=[("st", c)], writes=[("st", c)])

            def s2():
                S.op("pool", lambda e: e.tensor_tensor(out=ss, in0=ss, in1=nh[:, :], op=ALU.pow),
                     reads=[("st", c), "nh"], writes=[("st", c)])

            def s3():
                S.op("dve", lambda e: e.scalar_tensor_tensor(out=x_ap, in0=x_ap, scalar=ss, in1=gfin[:],
                                                             op0=ALU.mult, op1=ALU.mult),
                     reads=xkeys + [("st", c), "gfin"], writes=xkeys)

            def s4():
                S.op("sp", lambda e: e.dma_start(out=dst, in_=x_ap), reads=xkeys, dma=f"sty{slot}")
                ldstate["stores"] += 1
                try_loads()

            def s12():
                s1()
                s2()

            def s34():
                s3()
                s4()

            return add_chain([s00, s0, s12, s34])

        emit_halo_front()
        emit_N_norm(0, 0)
        emit_N_norm(0, 1)
        emit_N_norm(0, 2)
        tick()
        tick()
        emit_N_tr(0, 0)
        emit_N_norm(0, 3)
        tick()
        tick()
        emit_win_group(0)
        emit_N_tr(0, 1)
        emit_N_tr(0, 2)
        emit_win_group(1)
        emit_N_tr(0, 3)
        S.op("pool", lambda e: e.tensor_scalar(out=icntp[:, :], in0=icnt[:, 3, :], scalar1=16.0, scalar2=0.0,
                                               op0=ALU.mult, op1=ALU.add), reads=["icnt"], writes=["icnt"])
        emit_poolw_stage()
        try_loads(limit=6)

        order = [("B", 0), ("A", 0), ("B", 1), ("A", 1), ("B", 2), ("A", 2), ("B", 3), ("A", 3)]
        pendB = None
        for i in range(NT + 1):
            tile = tile_desc(i) if i < NT else None
            if i == NPS:
                while chains:
                    tick()
                emit_cache_loads()
                for gi, (kind, c) in enumerate(order):
                    if kind == "A":
                        mixer_A(c, tile)
                    else:
                        par = mixer_B_front(c, tile)
                    if pendB is not None:
                        mixer_B_back(pendB[0], pendB[2], pendB[1])
                        pendB = None
                    if kind == "B":
                        pendB = (c, par, tile)
                assert pendB is None
                L = [(i - 1, j) for j in range(nsub(i - 1))] + [(i, 0)]
                for s in range(len(L) + 3):
                    deferred = []
                    if 0 <= s - 3 < len(L):
                        emit_O_gate(*L[s - 3])
                        deferred.append(L[s - 3])
                    if 0 <= s - 2 < len(L):
                        emit_O_tr2(*L[s - 2])
                    if s < len(L):
                        emit_O_wout(*L[s])
                    for (di, dj) in deferred:
                        emit_O_fin(di, dj)
                    tick()
                break
            for gi, (kind, c) in enumerate(order):
                deferred = []
                if i == 0:
                    emit_win_group(gi + 2)
                    if kind == "A":
                        emit_halo_A(c)
                    else:
                        emit_halo_B(c)
                if tile is not None:
                    if kind == "A":
                        mixer_A(c, tile)
                    else:
                        par = mixer_B_front(c, tile)
                if pendB is not None:
                    mixer_B_back(pendB[0], pendB[2], pendB[1])
                    pendB = None
                if tile is not None and kind == "B":
                    pendB = (c, par, tile)
                if i == 0:
                    for k in {3: (0, 1, 2, 3), 4: (4, 5, 6, 7)}.get(gi, ()):
                        emit_wout_stage(k)
                    for k in {6: (0, 1, 2, 3), 7: (4, 5, 6, 7)}.get(gi, ()):
                        emit_wgate_stage(k)
                    if gi == 2:
                        try_loads(limit=8)
                    if gi == 7:
                        emit_wple_stage(0)
                        emit_wple_stage(1)
                        S.op("sp", lambda e: e.dma_start(out=gfin[:], in_=gfin_in), writes=["gfin"], dma="gfin")
                    if gi == 7:
                        try_loads()
                if i >= 1:
                    ns = nsub(i - 1)
                    if 0 <= gi - 3 < ns:
                        emit_O_gate(i - 1, gi - 3)
                        deferred.append((i - 1, gi - 3))
                    if 0 <= gi - 2 < ns:
                        emit_O_tr2(i - 1, gi - 2)
                    if gi < ns:
                        emit_O_wout(i - 1, gi)
                if i + 1 < NT:
                    nn = nsub(i + 1)
                    for (jn, g_norm, g_tr) in ((0, 0, 1), (1, 1, 2), (2, 4, 5), (3, 5, 6)):
                        if jn < nn:
                            if gi == g_tr:
                                emit_N_tr(i + 1, jn)
                    for (jn, g_norm, g_tr) in ((0, 0, 1), (1, 1, 2), (2, 4, 5), (3, 5, 6)):
                        if jn < nn:
                            if gi == g_norm:
                                emit_N_norm(i + 1, jn)
                for (di, dj) in deferred:
                    emit_O_fin(di, dj)
                if tile is None:
                    tick()
        while chains:
            tick()

        S.finalize()
        final_waits = [(n, S.dma_count[n]) for n in dma_names if n.startswith("sty") or n.startswith("state")
                       if S.dma_count.get(n, 0) > 0]
        with nc.Block() as block:
            @block.tensor
            def _(e):
                S.emit("pe", e, sems, dma_sems)

            @block.scalar
            def _(e):
                S.emit("act", e, sems, dma_sems)

            @block.vector
            def _(e):
                S.emit("dve", e, sems, dma_sems)

            @block.gpsimd
            def _(e):
                S.emit("pool", e, sems, dma_sems)

            @block.sync
            def _(e):
                S.emit("sp", e, sems, dma_sems)
                for n, v in final_waits:
                    e.wait_ge(dma_sems[n], v)
    return nc


def _prep_inputs(inp, NPS):
    f = lambda a: np.ascontiguousarray(np.asarray(a, dtype=np.float32))
    x_prompt, x_sample = f(inp["x_prompt"]), f(inp["x_sample"])
    p_prompt, p_sample = f(inp["p_prompt"])[0], f(inp["p_sample"])[0]
    cache_conv, cache_pool = f(inp["cache_conv"])[0], f(inp["cache_pool"])[0]
    NPTOK = NPS * 512
    B, SEQ, _ = x_prompt.shape
    cps = NCORES // B
    assert SEQ == cps * NPTOK
    cols = []
    cols += [f(inp["g_mix"])[0].reshape(8, 128)]
    cols += [f(inp["g_ple"])[0].reshape(8, 128)]
    cols += [f(inp["conv_w"])[0].reshape(12, 128)]
    cols += [f(inp["conv_b"])[0].reshape(4, 128)]
    cols += [f(inp["pool_scale"])[0].reshape(4, 128)]
    prm = np.ascontiguousarray(np.concatenate(cols, axis=0).T)
    gfin = np.ascontiguousarray(np.broadcast_to(f(inp["g_final"])[None, :], (128, D)))
    import ml_dtypes
    ident = np.eye(128, dtype=np.float32).astype(ml_dtypes.bfloat16)
    icnt_first = np.zeros((128, 4, 16), np.float32)
    icnt_rest = np.zeros((128, 4, 16), np.float32)
    for g in range(4):
        w = 2 ** (g + 1)
        icnt_first[:, g, :] = 1.0 / np.minimum(np.arange(16) + 1, w)
        icnt_rest[:, g, :] = 1.0 / w
    w_in_r = np.ascontiguousarray(f(inp["w_in"])[0].reshape(8, 128, 24, 128).transpose(2, 1, 0, 3).reshape(24, 128, D))
    shared = dict(w_in=w_in_r, w_out=f(inp["w_out"])[0], w_gate=f(inp["w_ple_gate"])[0],
                  w_ple=f(inp["w_ple"])[0], pool_w=f(inp["pool_w"])[0], prm=prm, gfin=gfin, ident=ident)
    maps = []
    for c in range(NCORES):
        b, ch = divmod(c, cps)
        s0 = ch * NPTOK
        xp = np.zeros((HALO + NPTOK, D), np.float32)
        xp[HALO:] = x_prompt[b, s0:s0 + NPTOK]
        if ch > 0:
            xp[:HALO] = x_prompt[b, s0 - HALO:s0]
        sl = slice(4 * c, 4 * c + 4)
        cc = cache_conv[sl].reshape(4, 2, 4, 128)
        cp = cache_pool[sl].reshape(4, 15, 4, 128)
        m = dict(shared)
        m.update(xp=xp, pp=np.ascontiguousarray(p_prompt[b, s0:s0 + NPTOK]),
                 xs=np.ascontiguousarray(x_sample[sl].reshape(128, D)),
                 ps=np.ascontiguousarray(p_sample[sl].reshape(128, PLE)),
                 cconvT=np.ascontiguousarray(cc.transpose(3, 2, 0, 1)),
                 cpoolT=np.ascontiguousarray(cp.transpose(3, 2, 0, 1)),
                 icnt=icnt_first if ch == 0 else icnt_rest)
        maps.append(m)
    return maps, (B, SEQ, cps)


def _assemble(results, meta, NPS):
    B, SEQ, cps = meta
    NPTOK = NPS * 512
    y_prompt = np.empty((B, SEQ, D), np.float32)
    y_sample = np.empty((32, 32, D), np.float32)
    sc_p = np.empty((1, B, 2, 512), np.float32)
    sp_p = np.empty((1, B, 15, 512), np.float32)
    sc_s = np.empty((1, 32, 2, 512), np.float32)
    sp_s = np.empty((1, 32, 15, 512), np.float32)
    for c, r in enumerate(results):
        b, ch = divmod(c, cps)
        y_prompt[b, ch * NPTOK:(ch + 1) * NPTOK] = r["yp"]
        y_sample[4 * c:4 * c + 4] = r["ys"].reshape(4, 32, D)
        sc_s[0, 4 * c:4 * c + 4] = r["sconv_s"].transpose(2, 3, 1, 0).reshape(4, 2, 512)
        sp_s[0, 4 * c:4 * c + 4] = r["spool_s"].transpose(2, 3, 1, 0).reshape(4, 15, 512)
        if ch == cps - 1:
            sc_p[0, b] = r["sconv_p"].transpose(2, 1, 0).reshape(2, 512)
            sp_p[0, b] = r["spool_p"].transpose(2, 1, 0).reshape(15, 512)
    return (y_prompt, y_sample, sc_p, sp_p, sc_s, sp_s)


_NC_CACHE = {}


def _run(inp, NPS):
    maps, meta = _prep_inputs(inp, NPS)
    if NPS not in _NC_CACHE:
        _NC_CACHE[NPS] = build(NPS)
    res = run_bass_kernel_spmd(_NC_CACHE[NPS], maps, core_ids=list(range(NCORES)))
    return _assemble(res.results, meta, NPS)


def kernel(**inputs):
    return _run(inputs, 8)
```

```python
import numpy as np
from contextlib import ExitStack
import concourse.bass as bass
import concourse.mybir as mybir
from concourse.bass_utils import run_bass_kernel_spmd

F32 = mybir.dt.float32
BF16 = mybir.dt.bfloat16
AF = mybir.ActivationFunctionType
ALU = mybir.AluOpType

D = 1024
PW = 3072
PLE = 256
EPS = 1e-6
HALO = 16
NCORES = 8


class Sched:
    ENGS = ("pe", "act", "dve", "pool", "sp")

    def __init__(self):
        self.ops = {e: [] for e in self.ENGS}
        self.last_writer = {}
        self.readers = {}
        self.prev_access = {}
        self.dma_count = {}
        self.total_wait = set()

    def op(self, eng, fn, reads=(), writes=(), dma=None):
        idx = len(self.ops[eng])
        deps = {}

        def add(d, raw):
            if d is None:
                return
            deps[d] = deps.get(d, False) or raw

        for k in reads:
            if isinstance(k, str) and k.startswith("PS:"):
                pa = self.prev_access.get(k)
                if pa is not None:
                    add((pa[0], pa[1]), pa[2])
            else:
                add(self.last_writer.get(k), True)
        for k in writes:
            if isinstance(k, str) and k.startswith("PS:"):
                pa = self.prev_access.get(k)
                if pa is not None:
                    add((pa[0], pa[1]), False)
            else:
                add(self.last_writer.get(k), False)
                for r in self.readers.get(k, ()):
                    add(r, False)
        rec = dict(fn=fn, deps=deps, dma=dma, signal=False, dma_val=None)
        if dma is not None:
            self.dma_count[dma] = self.dma_count.get(dma, 0) + 16
            rec["dma_val"] = self.dma_count[dma]
        self.ops[eng].append(rec)
        for k in reads:
            if isinstance(k, str) and k.startswith("PS:"):
                self.prev_access[k] = (eng, idx, False)
            else:
                self.readers.setdefault(k, []).append((eng, idx))
        for k in writes:
            if isinstance(k, str) and k.startswith("PS:"):
                self.prev_access[k] = (eng, idx, True)
            else:
                self.last_writer[k] = (eng, idx)
                self.readers[k] = []
        return (eng, idx)

    def finalize(self):
        for eng in self.ENGS:
            for rec in self.ops[eng]:
                keep = []
                for (de, di), raw in rec["deps"].items():
                    drec = self.ops[de][di]
                    if de == eng and drec["dma"] is None:
                        if eng == "pe":
                            continue
                    keep.append((de, di))
                    if drec["dma"] is None:
                        drec["signal"] = True
                rec["keep"] = keep
        for eng in self.ENGS:
            c = 0
            for rec in self.ops[eng]:
                if rec["dma"] is None and rec["signal"]:
                    c += 1
                    rec["sig_val"] = c

    def emit(self, eng, engine, sems, dma_sems):
        known = {}
        for rec in self.ops[eng]:
            waits = {}
            for (de, di) in rec["keep"]:
                drec = self.ops[de][di]
                if drec["dma"] is not None:
                    key = ("dma", drec["dma"])
                    val = drec["dma_val"]
                    if drec["dma"] in self.total_wait:
                        val = self.dma_count[drec["dma"]]
                else:
                    key = ("eng", de)
                    val = drec["sig_val"]
                if val > waits.get(key, 0):
                    waits[key] = val
            for key, val in waits.items():
                if known.get(key, 0) >= val:
                    continue
                known[key] = val
                sem = dma_sems[key[1]] if key[0] == "dma" else sems[key[1]]
                engine.wait_ge(sem, val)
            ins = rec["fn"](engine)
            if rec["dma"] is not None:
                ins.then_inc(dma_sems[rec["dma"]], 16)
            elif rec["signal"]:
                ins.then_inc(sems[eng], 1)


def build(NPS):
    NT = NPS + 1
    NPTOK = NPS * 512
    nc = bass.Bass("TRN2", target_bir_lowering=False)

    def din(name, shape):
        return nc.dram_tensor(name, shape, F32, kind="ExternalInput").ap()

    def dout(name, shape):
        return nc.dram_tensor(name, shape, F32, kind="ExternalOutput").ap()

    xp = din("xp", [HALO + NPTOK, D])
    pp = din("pp", [NPTOK, PLE])
    xs = din("xs", [128, D])
    ps_in = din("ps", [128, PLE])
    cconvT = din("cconvT", [128, 4, 4, 2])
    cpoolT = din("cpoolT", [128, 4, 4, 15])
    w_in = din("w_in", [24, 128, D])
    w_out = din("w_out", [D, D])
    w_gate = din("w_gate", [D, D])
    w_ple = din("w_ple", [PLE, D])
    pool_w = din("pool_w", [4, 128, 128])
    prm_in = din("prm", [128, 36])
    gfin_in = din("gfin", [128, D])
    icnt_in = din("icnt", [128, 4, 16])
    ident_in = nc.dram_tensor("ident", [128, 128], BF16, kind="ExternalInput").ap()

    yp = dout("yp", [NPTOK, D])
    ys = dout("ys", [128, D])
    sconv_p = dout("sconv_p", [128, 4, 2])
    spool_p = dout("spool_p", [128, 4, 15])
    sconv_s = dout("sconv_s", [128, 4, 4, 2])
    spool_s = dout("spool_s", [128, 4, 4, 15])

    S = Sched()
    es = ExitStack()
    with es:
        def sb(name, shape, dt):
            return es.enter_context(nc.sbuf_tensor(name, shape, dt))

        def psum(name, shape, dt):
            return es.enter_context(nc.psum_tensor(name, shape, dt))

        NXB = 10
        NPB = 1
        xh = sb("xh", [128, NXB, D], F32)
        pin = sb("pin", [128, NPB, PLE], F32)
        winb = sb("winb", [128, 8, PW], BF16)
        woutb = sb("woutb", [128, 8, D], BF16)
        wgb = sb("wgb", [128, 8, D], BF16)
        wpb = sb("wpb", [128, 2, D], BF16)
        pwb = sb("pwb", [128, 4, 128], BF16)
        prm = sb("prm_sb", [128, 36], F32)
        gfin = sb("gfin_sb", [128, D], F32)
        icnt = sb("icnt_sb", [128, 4, 16], F32)
        idb = sb("idb", [128, 128], BF16)
        nh = sb("nh", [128, 1], F32)
        NHN = 3
        NPT = 3
        hn = [sb(f"hn{i}", [128, D], BF16) for i in range(NHN)]
        pb = [sb(f"pb{i}", [128, PLE], BF16) for i in range(1)]
        hnT = [sb(f"hnT{i}", [128, 8, 512], BF16) for i in range(2)]
        hnTh = sb("hnTh", [128, 8, HALO], BF16)
        yT = [sb(f"yT{i}", [128, 8, 512], BF16) for i in range(2)]
        hn2T = [sb(f"hn2T{i}", [128, 8, 128], BF16) for i in range(2)]
        pT = [sb(f"pT{i}", [128, 2, 128], BF16) for i in range(NPT)]
        th = [sb(f"th{i}", [128, D], F32) for i in range(1)]
        xhalo = th[0][0:HALO, :]
        UP = [sb(f"UP{c}", [128, 1, 2 + 512], F32) for c in range(4)]
        VP = [sb(f"VP{g}", [128, 1, 16 + 512], F32) for g in range(4)]
        US = [UP[c][:, 0, 0:4 * 34].rearrange("p (s t) -> p s t", s=4) for c in range(4)]
        VS = [VP[g][:, 0, 0:4 * 48].rearrange("p (s t) -> p s t", s=4) for g in range(4)]
        NPAR = 1
        tA2 = [sb(f"tA2_{i}", [128, 512], F32) for i in range(NPAR)]
        tA3 = [sb(f"tA3_{i}", [128, 512], F32) for i in range(NPAR)]
        tB1 = [sb(f"tB1_{i}", [128, 512], F32) for i in range(NPAR)]
        sA = [sb(f"sA_{i}", [128, 16 + 512], F32) for i in range(NPAR)]
        sB = [sb(f"sB_{i}", [128, 16 + 512], F32) for i in range(NPAR)]
        tB3 = [sb(f"tB3_{i}", [128, 512], BF16) for i in range(NPAR)]
        t16 = sb("t16", [128, 16], F32)
        t16h = sb("t16h", [128, 16], F32)
        icntp = sb("icntp", [128, 16], F32)
        NST = 3 * (NT * 4 + 2) + 8
        st = sb("stat", [128, NST], F32)

        TR = psum("TR", [128, 1024], BF16)
        NF, NTB = 4, 3
        FB = [psum(f"FB{i}", [128, 512], F32) for i in range(NF)]
        TB = [psum(f"TB{i}", [128, 512], F32) for i in range(NTB)]

        sems = {e: es.enter_context(nc.semaphore("s_" + e)) for e in Sched.ENGS}
        dma_names = ([f"ldx{i}" for i in range(NXB)] + [f"ldp{i}" for i in range(NPB)]
                     + [f"sty{i}" for i in range(NXB)] + [f"stg{i}" for i in range(4)]
                     + ["misc", "halo", "statep", "states", "cache", "gfin", "wo"])
        dma_sems = {n: es.enter_context(nc.semaphore("d_" + n)) for n in dma_names}
        S.total_wait.add("misc")
        S.total_wait.add("cache")
        S.total_wait.add("wo")
        S.total_wait.add("statep")
        S.total_wait.add("states")

        ctr = dict(stat=0, f=0, t=0, hn=0, par=0)

        def stat_alloc(n):
            c = ctr["stat"]
            ctr["stat"] += n
            assert ctr["stat"] <= NST
            return c

        def falloc():
            b = ctr["f"] % NF
            ctr["f"] += 1
            return b

        def talloc():
            b = ctr["t"] % NTB
            ctr["t"] += 1
            return b

        class Ring:
            def __init__(self, n, name):
                self.free = list(range(n))
                self.name = name

            def alloc(self):
                assert self.free, f"ring {self.name} exhausted"
                return self.free.pop(0)

            def release(self, s):
                assert s not in self.free
                self.free.append(s)

        hn_ring = Ring(NHN, "hn")
        pT_ring = Ring(NPT, "pT")
        hn2T_ring = Ring(2, "hn2T")

        def hnalloc():
            return hn_ring.alloc()

        S.op("sp", lambda e: e.dma_start(out=prm[:], in_=prm_in), writes=["prm"], dma="misc")
        S.op("sp", lambda e: e.dma_start(out=idb[:], in_=ident_in), writes=["idb"], dma="misc")
        S.op("sp", lambda e: e.dma_start(out=xhalo, in_=xp[0:HALO, :]), writes=[("th", 0, 0), ("th", 0, 1)], dma="halo")

        slot_of = {}
        ldstate = dict(next=0, stores=0)

        def xslot(i, j):
            return slot_of[(i, j)]

        def nsub(i):
            return 4 if i < NPS else 1

        olist = [(i, j) for i in range(NT) for j in range(nsub(i))]
        oindex = {ij: q for q, ij in enumerate(olist)}

        def emit_load(q):
            i, j = olist[q]
            slot = q % NXB
            slot_of[(i, j)] = slot
            if i < NPS:
                r0 = (i * 4 + j) * 128
                xsrc = xp[HALO + r0:HALO + r0 + 128, :]
            else:
                xsrc = xs
            S.op("sp", lambda e: e.dma_start(out=xh[:, slot, :], in_=xsrc), writes=[("xh", slot)], dma=f"ldx{slot}")

        def emit_pload(q):
            if q >= len(olist):
                return
            i, j = olist[q]
            ps_ = q % NPB
            if i < NPS:
                r0 = (i * 4 + j) * 128
                psrc = pp[r0:r0 + 128, :]
            else:
                psrc = ps_in
            S.op("sp", lambda e: e.dma_start(out=pin[:, ps_, :], in_=psrc), writes=[("pin", ps_)], dma=f"ldp{ps_}")

        def try_loads(limit=None):
            while ldstate["next"] < len(olist) and ldstate["next"] < ldstate["stores"] + NXB:
                if limit is not None and ldstate["next"] >= limit:
                    break
                emit_load(ldstate["next"])
                ldstate["next"] += 1

        try_loads(limit=4)
        S.op("sp", lambda e: e.dma_start(out=icnt[:], in_=icnt_in), writes=["icnt"], dma="misc")
        emit_pload(0)

        def emit_cache_loads():
            for c in range(4):
                S.op("sp", lambda e, c=c: e.dma_start(out=US[c][:, :, 0:2], in_=cconvT[:, c, :, :]),
                     writes=[("U", c)], dma="cache")
                S.op("sp", lambda e, c=c: e.dma_start(out=VS[c][:, :, 1:16], in_=cpoolT[:, c, :, :]),
                     writes=[("V", c)], dma="cache")
        S.op("pool", lambda e: e.memset(nh[:], -0.5), writes=["nh"])

        yT1f = yT[1][:].bitcast(F32).rearrange("p k n -> p (k n)")
        stg_views = [xh[:, NXB - 1, :], th[0][:, :], yT1f[:, 0:1024], yT1f[:, 1024:2048]]
        stg_keys = [[("xh", NXB - 1)], [("th", 0, 0), ("th", 0, 1)],
                    [(("yT", 1), c) for c in range(4)], [(("yT", 1), c) for c in range(4, 8)]]
        stg_ctr = [0]

        def stage(src_ap, shape3, consume):
            i = stg_ctr[0] % 4
            stg_ctr[0] += 1
            v = stg_views[i]
            if shape3 is not None:
                ncol = shape3[0] * shape3[1]
                v = v[:, 0:ncol].rearrange("p (a n) -> p a n", a=shape3[0])
            S.op("sp", lambda e: e.dma_start(out=v, in_=src_ap), writes=stg_keys[i], dma=f"stg{i}")
            consume(v, stg_keys[i])

        chunk_groups = []
        for c in range(4):
            chunk_groups.append([16 + c, 20 + c])
            chunk_groups.append([c, 8 + c, 12 + c, 4 + c])
        gmix_b = prm[:, 0:8].unsqueeze(2).to_broadcast([128, 8, 128])
        win_ctr = [0]

        def emit_win_group(gidx):
            if gidx >= len(chunk_groups):
                return
            for cc in chunk_groups[gidx]:
                eng = "dve"
                win_ctr[0] += 1

                def consume(v, keys, cc=cc, eng=eng):
                    S.op(eng, lambda e: e.tensor_tensor(out=winb[:, :, cc * 128:(cc + 1) * 128], in0=v, in1=gmix_b,
                                                        op=ALU.mult),
                         reads=keys + ["prm"], writes=[("winb", cc)])
                stage(w_in[cc].rearrange("p (k n) -> p k n", k=8), (8, 128), consume)

        def winb_key(k, c):
            return ("winb", c)

        def emit_wout_stage(k):
            S.op("pool", lambda e: e.dma_start(out=woutb[:, k, :], in_=w_out[k * 128:(k + 1) * 128, :]),
                 writes=[("woutb", k)], dma="wo")

        def emit_wgate_stage(k):
            def consume(v, keys):
                S.op("act", lambda e: e.activation(out=wgb[:, k, :], in_=v, func=AF.Copy, scale=prm[:, 8 + k:9 + k]),
                     reads=keys + ["prm"], writes=[("wgb", k)])
            stage(w_gate[k * 128:(k + 1) * 128, :], None, consume)

        def emit_wple_stage(k):
            def consume(v, keys):
                S.op("act", lambda e: e.activation(out=wpb[:, k, :], in_=v, func=AF.Copy, scale=0.5),
                     reads=keys, writes=[("wpb", k)])
            stage(w_ple[k * 128:(k + 1) * 128, :], None, consume)

        def emit_poolw_stage():
            def consume(v, keys):
                S.op("act", lambda e: e.activation(out=pwb[:], in_=v, func=AF.Copy), reads=keys, writes=["pwb"])
            stage(pool_w.rearrange("g c d -> c g d"), (4, 128), consume)

        chains = []

        def add_chain(stages):
            ch = list(stages)
            ch.pop(0)()
            if ch:
                chains.append(ch)
            return ch

        def tick():
            for ch in list(chains):
                ch.pop(0)()
                if not ch:
                    chains.remove(ch)

        def finish(ch):
            while ch:
                ch.pop(0)()
            if ch in chains:
                chains.remove(ch)

        def emit_norm(x_ap, ntok, xkeys):
            c = stat_alloc(1)
            h = hnalloc()
            ss = st[0:ntok, c:c + 1]
            ch = add_chain([
                lambda: S.op("act", lambda e: e.activation(out=hn[h][0:ntok, :], in_=x_ap, func=AF.Square, accum_out=ss),
                             reads=xkeys, writes=[("st", c), ("hn", h)]),
                lambda: (S.op("dve", lambda e: e.tensor_scalar(out=ss, in0=ss, scalar1=1.0 / D, scalar2=EPS, op0=ALU.mult,
                                                               op1=ALU.add), reads=[("st", c)], writes=[("st", c)]),
                         S.op("pool", lambda e: e.tensor_tensor(out=ss, in0=ss, in1=nh[0:ntok, :], op=ALU.pow),
                              reads=[("st", c), "nh"], writes=[("st", c)])),
                lambda: S.op("act", lambda e: e.activation(out=hn[h][0:ntok, :], in_=x_ap, func=AF.Copy, scale=ss),
                             reads=xkeys + [("st", c)], writes=[("hn", h)]),
            ])
            return h, ch

        def emit_transposes(src_tile, nch, ntok, src_keys, dst_ap, dst_keys):
            for k in range(nch):
                S.op("pe", lambda e, k=k: e.transpose(out=TR[:, k * 128:k * 128 + ntok],
                                                     in_=src_tile[0:ntok, k * 128:(k + 1) * 128],
                                                     identity=idb[0:ntok, 0:ntok]),
                     reads=src_keys + ["idb"], writes=["PS:TR"])
            src = TR[:, 0:nch * 128].rearrange("p (k t) -> p k t", k=nch)[:, :, 0:ntok]
            S.op("dve", lambda e: e.tensor_copy(out=dst_ap, in_=src), reads=["PS:TR"], writes=dst_keys)

        def proj_chunk(c, src_ap, src_keys, n):
            b = falloc()
            for k in range(8):
                S.op("pe", lambda e, k=k: e.matmul(FB[b][:, 0:n], lhsT=winb[:, k, c * 128:(c + 1) * 128],
                                                   rhs=src_ap[:, k, :], start=(k == 0), stop=(k == 7)),
                     reads=src_keys + [winb_key(k, c)], writes=[f"PS:F{b}"])
            tick()
            return b

        def cw(jj, c):
            col = 16 + jj * 4 + c
            return prm[:, col:col + 1]

        def cbias(c):
            return prm[:, 28 + c:29 + c]

        def pscale(g):
            return prm[:, 32 + g:33 + g]

        def mixer_A(c, tile):
            S_, T_, n = tile["S"], tile["T"], tile["n"]
            U = tile["U"][c]
            ukey = ("U", c)
            src, skeys = tile["hnT"], tile["hnT_keys"]
            par = 0

            def v3(ap):
                return ap.rearrange("p (s t) -> p s t", s=S_)

            bh = proj_chunk(c, src, skeys, n)
            S.op("act", lambda e: e.activation(out=U[:, :, 2:2 + T_], in_=v3(FB[bh][:, 0:n]), func=AF.Copy),
                 reads=[f"PS:F{bh}"], writes=[ukey])
            bc = proj_chunk(8 + c, src, skeys, n)
            S.op("dve", lambda e: e.tensor_tensor(out=U[:, :, 2:2 + T_], in0=v3(FB[bc][:, 0:n]),
                                                  in1=U[:, :, 2:2 + T_], op=ALU.mult),
                 reads=[f"PS:F{bc}", ukey], writes=[ukey])
            bz = proj_chunk(12 + c, src, skeys, n)
            S.op("act", lambda e: e.activation(out=tA2[par][:, 0:n], in_=FB[bz][:, 0:n], func=AF.Silu),
                 reads=[f"PS:F{bz}"], writes=[("tA2", par)])
            t3 = v3(tA3[par][:, 0:n])
            S.op("act", lambda e: e.activation(out=t3, in_=U[:, :, 0:T_], func=AF.Identity, scale=cw(0, c), bias=cbias(c)),
                 reads=[ukey, "prm"], writes=[("tA3", par)])
            S.op("dve", lambda e: e.scalar_tensor_tensor(out=t3, in0=U[:, :, 1:1 + T_], scalar=cw(1, c), in1=t3,
                                                         op0=ALU.mult, op1=ALU.add),
                 reads=[ukey, ("tA3", par), "prm"], writes=[("tA3", par)])
            S.op("dve", lambda e: e.scalar_tensor_tensor(out=t3, in0=U[:, :, 2:2 + T_], scalar=cw(2, c), in1=t3,
                                                         op0=ALU.mult, op1=ALU.add),
                 reads=[ukey, ("tA3", par), "prm"], writes=[("tA3", par)])
            bb = proj_chunk(4 + c, src, skeys, n)
            S.op("dve", lambda e: e.tensor_tensor(out=tA3[par][:, 0:n], in0=FB[bb][:, 0:n], in1=tA3[par][:, 0:n],
                                                  op=ALU.mult),
                 reads=[f"PS:F{bb}", ("tA3", par)], writes=[("tA3", par)])
            S.op("pool", lambda e: e.tensor_tensor(out=tile["yT"][:, c, :], in0=tA3[par][:, 0:n], in1=tA2[par][:, 0:n],
                                                   op=ALU.mult),
                 reads=[("tA3", par), ("tA2", par)], writes=[(tile["yT_key"], c)])
            if tile["kind"] == "p":
                if tile["last"]:
                    S.op("sp", lambda e: e.dma_start(out=sconv_p[:, c, :], in_=U[:, 0, T_:T_ + 2]),
                         reads=[ukey], dma="statep")
                else:
                    S.op("pool", lambda e: e.tensor_copy(out=U[:, :, 0:2], in_=U[:, :, T_:T_ + 2]),
                         reads=[ukey], writes=[ukey])
            else:
                S.op("sp", lambda e: e.dma_start(out=sconv_s[:, c, :, :], in_=U[:, :, T_:T_ + 2]),
                     reads=[ukey], dma="states")

        def mixer_B_front(g, tile):
            S_, T_, n = tile["S"], tile["T"], tile["n"]
            V = tile["V"][g]
            vkey = ("V", g)
            src, skeys = tile["hnT"], tile["hnT_keys"]
            par = 0
            W = 2 ** (g + 1)

            def v3(ap):
                return ap.rearrange("p (s t) -> p s t", s=S_)

            def sv(buf):
                return buf[:, 0:S_ * (16 + T_)].rearrange("p (s t) -> p s t", s=S_)

            bv = proj_chunk(16 + g, src, skeys, n)
            S.op("act", lambda e: e.activation(out=V[:, :, 16:16 + T_], in_=v3(FB[bv][:, 0:n]), func=AF.Copy),
                 reads=[f"PS:F{bv}"], writes=[vkey])
            bz = proj_chunk(20 + g, src, skeys, n)
            S.op("act", lambda e: e.activation(out=tB1[par][:, 0:n], in_=FB[bz][:, 0:n], func=AF.Silu),
                 reads=[f"PS:F{bz}"], writes=[("tB1", par)])
            a3, b3 = sv(sA[par]), sv(sB[par])
            aeng = "dve" if g < 3 else "pool"
            stages = []
            lo = -(W - 2)
            stages.append(lambda lo=lo: S.op(aeng, lambda e: e.tensor_tensor(
                out=a3[:, :, 16 + lo:16 + T_], in0=V[:, :, 16 + lo:16 + T_], in1=V[:, :, 15 + lo:15 + T_], op=ALU.add),
                reads=[vkey], writes=[("sA", par)]))
            fin, fkey = a3, ("sA", par)
            if W >= 4:
                lo = -(W - 4)
                stages.append(lambda lo=lo: S.op(aeng, lambda e: e.tensor_tensor(
                    out=b3[:, :, 16 + lo:16 + T_], in0=a3[:, :, 16 + lo:16 + T_], in1=a3[:, :, 14 + lo:14 + T_], op=ALU.add),
                    reads=[("sA", par)], writes=[("sB", par)]))
                fin, fkey = b3, ("sB", par)
            if W >= 8:
                lo = -(W - 8)
                stages.append(lambda lo=lo: S.op(aeng, lambda e: e.tensor_tensor(
                    out=a3[:, :, 16 + lo:16 + T_], in0=b3[:, :, 16 + lo:16 + T_], in1=b3[:, :, 12 + lo:12 + T_], op=ALU.add),
                    reads=[("sB", par)], writes=[("sA", par)]))
                fin, fkey = a3, ("sA", par)
            if W >= 16:
                stages.append(lambda: S.op(aeng, lambda e: e.tensor_tensor(
                    out=b3[:, :, 16:16 + T_], in0=a3[:, :, 16:16 + T_], in1=a3[:, :, 8:8 + T_], op=ALU.add),
                    reads=[("sA", par)], writes=[("sB", par)]))
                fin, fkey = b3, ("sB", par)

            def s_pooled(fin=fin, fkey=fkey):
                if aeng == "dve":
                    S.op("dve", lambda e: e.scalar_tensor_tensor(out=v3(tB3[par][:, 0:n]), in0=fin[:, :, 16:16 + T_],
                                                                 scalar=1.0 / W, in1=V[:, :, 16:16 + T_],
                                                                 op0=ALU.mult, op1=ALU.subtract),
                         reads=[fkey, vkey], writes=[("tB3", par)])
                else:
                    S.op("pool", lambda e: e.tensor_scalar(out=fin[:, :, 16:16 + T_], in0=fin[:, :, 16:16 + T_],
                                                           scalar1=1.0 / W, scalar2=0.0, op0=ALU.mult, op1=ALU.add),
                         reads=[fkey], writes=[fkey])
                    S.op("pool", lambda e: e.tensor_tensor(out=v3(tB3[par][:, 0:n]), in0=fin[:, :, 16:16 + T_],
                                                           in1=V[:, :, 16:16 + T_], op=ALU.subtract),
                         reads=[fkey, vkey], writes=[("tB3", par)])
                if tile["first"]:
                    S.op(aeng, lambda e: e.tensor_tensor(out=t16[:], in0=fin[:, 0, 16:32],
                                                         in1=(icnt[:, g, :] if aeng == "dve" else icntp[:, :]), op=ALU.mult),
                         reads=[fkey, "icnt"], writes=["t16"])
                    S.op(aeng, lambda e: e.tensor_tensor(out=tB3[par][:, 0:16], in0=t16[:], in1=V[:, 0, 16:32],
                                                         op=ALU.subtract),
                         reads=["t16", vkey], writes=[("tB3", par)])
            adds = stages
            stages = []
            for a_ in range(0, len(adds), 2):
                grp = adds[a_:a_ + 2]
                stages.append(lambda grp=grp: [f_() for f_ in grp])
            stages.append(s_pooled)
            def s_hist():
                if tile["kind"] == "p":
                    if tile["last"]:
                        S.op("sp", lambda e: e.dma_start(out=spool_p[:, g, :], in_=V[:, 0, T_ + 1:T_ + 16]),
                             reads=[vkey], dma="statep")
                    else:
                        S.op("pool", lambda e: e.tensor_copy(out=V[:, :, 0:16], in_=V[:, :, T_:T_ + 16]),
                             reads=[vkey], writes=[vkey])
                else:
                    S.op("sp", lambda e: e.dma_start(out=spool_s[:, g, :, :], in_=V[:, :, T_ + 1:T_ + 16]),
                         reads=[vkey], dma="states")
            stages.append(s_hist)
            bch = add_chain(stages)
            return par, bch

        def mixer_B_back(g, tile, par_ch):
            par, bch = par_ch
            finish(bch)
            n = tile["n"]
            bm = falloc()
            S.op("pe", lambda e: e.matmul(FB[bm][:, 0:n], lhsT=pwb[:, g, :], rhs=tB3[par][:, 0:n], start=True, stop=True),
                 reads=[("tB3", par), "pwb"], writes=[f"PS:F{bm}"])
            tick()
            S.op("dve", lambda e: e.scalar_tensor_tensor(out=tile["yT"][:, 4 + g, :], in0=FB[bm][:, 0:n], scalar=pscale(g),
                                                         in1=tB1[par][:, 0:n], op0=ALU.mult, op1=ALU.mult),
                 reads=[f"PS:F{bm}", ("tB1", par), "prm"], writes=[(tile["yT_key"], 4 + g)])

        def emit_halo_front():
            h, ch = emit_norm(xhalo, HALO, [("th", 0, 0), ("th", 0, 1)])
            finish(ch)
            emit_transposes(hn[h], 8, HALO, [("hn", h)], hnTh[:], ["hnTh"])
            hn_ring.release(h)

        def emit_halo_A(c):
            bh = proj_chunk(c, hnTh, ["hnTh"], HALO)
            S.op("act", lambda e: e.activation(out=t16h[:, 0:HALO], in_=FB[bh][:, 0:HALO], func=AF.Copy),
                 reads=[f"PS:F{bh}"], writes=["t16h"])
            bc = proj_chunk(8 + c, hnTh, ["hnTh"], HALO)
            S.op("dve", lambda e: e.tensor_tensor(out=UP[c][:, 0, 0:2], in0=FB[bc][:, HALO - 2:HALO],
                                                  in1=t16h[:, HALO - 2:HALO], op=ALU.mult),
                 reads=[f"PS:F{bc}", "t16h"], writes=[("U", c)])

        def emit_halo_B(g):
            bv = proj_chunk(16 + g, hnTh, ["hnTh"], HALO)
            S.op("act", lambda e: e.activation(out=VP[g][:, 0, 0:16], in_=FB[bv][:, 0:HALO], func=AF.Copy),
                 reads=[f"PS:F{bv}"], writes=[("V", g)])

        def tile_desc(i):
            par = i % 2
            if i < NPS:
                return dict(kind="p", S=1, T=512, n=512, U=UP, V=VP, hnT=hnT[par][:, :, :], hnT_keys=[("hnT", par)],
                            yT=yT[par][:, :, :], yT_key=("yT", par), first=(i == 0), last=(i == NPS - 1), idx=i)
            return dict(kind="s", S=4, T=32, n=128, U=US, V=VS, hnT=hnT[par][:, :, 0:128], hnT_keys=[("hnT", par)],
                        yT=yT[par][:, :, 0:128], yT_key=("yT", par), first=False, last=False, idx=i)

        nstate = {}

        def emit_N_norm(i, j):
            while (i, j) not in slot_of and chains:
                tick()
            slot = xslot(i, j)
            nstate[(i, j)] = emit_norm(xh[:, slot, :], 128, [("xh", slot)])

        def emit_N_tr(i, j):
            par = i % 2
            h, ch = nstate[(i, j)]
            finish(ch)
            emit_transposes(hn[h], 8, 128, [("hn", h)], hnT[par][:, :, j * 128:(j + 1) * 128], [("hnT", par)])
            hn_ring.release(h)

        def emit_N(i, j):
            emit_N_norm(i, j)
            emit_N_tr(i, j)

        ostate = {}

        def emit_pcast(q):
            if q >= len(olist):
                return
            S.op("act", lambda e: e.activation(out=pb[0][:], in_=pin[:, q % NPB, :], func=AF.Copy),
                 reads=[("pin", q % NPB)], writes=[("pb", 0)])

        def emit_O_wout(i, j):
            slot = xslot(i, j)
            par = i % 2
            b0, b1 = talloc(), talloc()
            ykeys = [(("yT", par), c) for c in range(8)]
            for k in range(8):
                for hf, b in ((0, b0), (1, b1)):
                    S.op("pe", lambda e, k=k, hf=hf, b=b: e.matmul(TB[b][:, :], lhsT=yT[par][:, k, j * 128:(j + 1) * 128],
                                                                   rhs=woutb[:, k, hf * 512:(hf + 1) * 512],
                                                                   start=(k == 0), stop=(k == 7)),
                         reads=ykeys + [("woutb", k)], writes=[f"PS:T{b}"])
            tick()
            q_o = oindex[(i, j)]
            pslot = pT_ring.alloc()
            pbslot = 0
            emit_pcast(q_o)
            emit_transposes(pb[pbslot], 2, 128, [("pb", pbslot)], pT[pslot][:], [("pT", pslot)])
            emit_pload(q_o + 1)
            for hf, b in ((0, b0), (1, b1)):
                S.op("dve", lambda e, hf=hf, b=b: e.tensor_tensor(out=xh[:, slot, hf * 512:(hf + 1) * 512], in0=TB[b][:, :],
                                                                  in1=xh[:, slot, hf * 512:(hf + 1) * 512], op=ALU.add),
                     reads=[f"PS:T{b}", ("xh", slot)], writes=[("xh", slot)])
            h, ch = emit_norm(xh[:, slot, :], 128, [("xh", slot)])
            ostate[(i, j)] = dict(h=h, pslot=pslot, ch=ch)

        def emit_O_tr2(i, j):
            o = ostate[(i, j)]
            finish(o["ch"])
            q = hn2T_ring.alloc()
            emit_transposes(hn[o["h"]], 8, 128, [("hn", o["h"])], hn2T[q][:], [("hn2T", q)])
            hn_ring.release(o["h"])
            o["q"] = q

        def emit_O_gate(i, j):
            o = ostate[(i, j)]
            slot = xslot(i, j)
            q, pslot = o["q"], o["pslot"]
            g0, g1 = talloc(), talloc()
            for k in range(8):
                for hf, b in ((0, g0), (1, g1)):
                    S.op("pe", lambda e, k=k, hf=hf, b=b: e.matmul(TB[b][:, :], lhsT=hn2T[q][:, k, :],
                                                                   rhs=wgb[:, k, hf * 512:(hf + 1) * 512],
                                                                   start=(k == 0), stop=(k == 7)),
                         reads=[("hn2T", q), ("wgb", k)], writes=[f"PS:T{b}"])
            tick()
            tq = 0
            for hf, b in ((0, g0), (1, g1)):
                S.op("act", lambda e, hf=hf, b=b: e.activation(out=th[tq][:, hf * 512:(hf + 1) * 512], in_=TB[b][:, :],
                                                               func=AF.Tanh, scale=0.5),
                     reads=[f"PS:T{b}"], writes=[("th", tq, hf)])
            p0, p1 = talloc(), talloc()
            for k in range(2):
                for hf, b in ((0, p0), (1, p1)):
                    S.op("pe", lambda e, k=k, hf=hf, b=b: e.matmul(TB[b][:, :], lhsT=pT[pslot][:, k, :],
                                                                   rhs=wpb[:, k, hf * 512:(hf + 1) * 512],
                                                                   start=(k == 0), stop=(k == 1)),
                         reads=[("pT", pslot), ("wpb", k)], writes=[f"PS:T{b}"])
            tick()
            for hf, b in ((0, p0), (1, p1)):
                S.op("dve", lambda e, hf=hf, b=b: e.scalar_tensor_tensor(out=th[tq][:, hf * 512:(hf + 1) * 512],
                                                                         in0=th[tq][:, hf * 512:(hf + 1) * 512], scalar=1.0,
                                                                         in1=TB[b][:, :], op0=ALU.add, op1=ALU.mult),
                     reads=[f"PS:T{b}", ("th", tq, hf)], writes=[("th", tq, hf)])
            hn2T_ring.release(q)
            pT_ring.release(pslot)

        def emit_O_fin(i, j, h2eng="pool"):
            slot = xslot(i, j)
            x_ap = xh[:, slot, :]
            xkeys = [("xh", slot)]
            c = stat_alloc(1)
            ss = st[:, c:c + 1]
            if i < NPS:
                r0 = (i * 4 + j) * 128
                dst = yp[r0:r0 + 128, :]
            else:
                dst = ys

            def s00():
                S.op(h2eng, lambda e: e.tensor_tensor(out=x_ap, in0=x_ap, in1=th[0][:], op=ALU.add),
                     reads=xkeys + [("th", 0, 0), ("th", 0, 1)], writes=xkeys)

            def s0():
                S.op("act", lambda e: e.activation(out=th[0][:, :], in_=x_ap, func=AF.Square, accum_out=ss),
                     reads=xkeys, writes=[("st", c), ("th", 0, 0), ("th", 0, 1)])

            def s1():
                S.op("dve", lambda e: e.tensor_scalar(out=ss, in0=ss, scalar1=1.0 / D, scalar2=EPS, op0=ALU.mult,
                                                      op1=ALU.add), reads=[("st", c)], writes=[("st", c)])

            def s2():
                S.op("pool", lambda e: e.tensor_tensor(out=ss, in0=ss, in1=nh[:, :], op=ALU.pow),
                     reads=[("st", c), "nh"], writes=[("st", c)])

            def s3():
                S.op("dve", lambda e: e.scalar_tensor_tensor(out=x_ap, in0=x_ap, scalar=ss, in1=gfin[:],
                                                             op0=ALU.mult, op1=ALU.mult),
                     reads=xkeys + [("st", c), "gfin"], writes=xkeys)

            def s4():
                S.op("sp", lambda e: e.dma_start(out=dst, in_=x_ap), reads=xkeys, dma=f"sty{slot}")
                ldstate["stores"] += 1
                try_loads()

            def s12():
                s1()
                s2()

            def s34():
                s3()
                s4()

            return add_chain([s00, s0, s12, s34])

        emit_halo_front()
        emit_N_norm(0, 0)
        emit_N_norm(0, 1)
        emit_N_norm(0, 2)
        tick()
        tick()
        emit_N_tr(0, 0)
        emit_N_norm(0, 3)
        tick()
        tick()
        emit_win_group(0)
        emit_N_tr(0, 1)
        emit_N_tr(0, 2)
        emit_win_group(1)
        emit_N_tr(0, 3)
        S.op("pool", lambda e: e.tensor_scalar(out=icntp[:, :], in0=icnt[:, 3, :], scalar1=16.0, scalar2=0.0,
                                               op0=ALU.mult, op1=ALU.add), reads=["icnt"], writes=["icnt"])
        emit_poolw_stage()
        try_loads(limit=6)

        order = [("B", 0), ("A", 0), ("B", 1), ("A", 1), ("B", 2), ("A", 2), ("B", 3), ("A", 3)]
        pendB = None
        for i in range(NT + 1):
            tile = tile_desc(i) if i < NT else None
            if i == NPS:
                while chains:
                    tick()
                emit_cache_loads()
                for gi, (kind, c) in enumerate(order):
                    if kind == "A":
                        mixer_A(c, tile)
                    else:
                        par = mixer_B_front(c, tile)
                    if pendB is not None:
                        mixer_B_back(pendB[0], pendB[2], pendB[1])
                        pendB = None
                    if kind == "B":
                        pendB = (c, par, tile)
                assert pendB is None
                L = [(i - 1, j) for j in range(nsub(i - 1))] + [(i, 0)]
                for s in range(len(L) + 3):
                    deferred = []
                    if 0 <= s - 3 < len(L):
                        emit_O_gate(*L[s - 3])
                        deferred.append(L[s - 3])
                    if 0 <= s - 2 < len(L):
                        emit_O_tr2(*L[s - 2])
                    if s < len(L):
                        emit_O_wout(*L[s])
                    for (di, dj) in deferred:
                        emit_O_fin(di, dj, h2eng="dve")
                    tick()
                break
            for gi, (kind, c) in enumerate(order):
                deferred = []
                if i == 0:
                    emit_win_group(gi + 2)
                    if kind == "A":
                        emit_halo_A(c)
                    else:
                        emit_halo_B(c)
                if tile is not None:
                    if kind == "A":
                        mixer_A(c, tile)
                    else:
                        par = mixer_B_front(c, tile)
                if pendB is not None:
                    mixer_B_back(pendB[0], pendB[2], pendB[1])
                    pendB = None
                if tile is not None and kind == "B":
                    pendB = (c, par, tile)
                if i == 0:
                    for k in {3: (0, 1, 2, 3), 4: (4, 5, 6, 7)}.get(gi, ()):
                        emit_wout_stage(k)
                    for k in {6: (0, 1, 2, 3), 7: (4, 5, 6, 7)}.get(gi, ()):
                        emit_wgate_stage(k)
                    if gi == 2:
                        try_loads(limit=8)
                    if gi == 7:
                        emit_wple_stage(0)
                        emit_wple_stage(1)
                        S.op("sp", lambda e: e.dma_start(out=gfin[:], in_=gfin_in), writes=["gfin"], dma="gfin")
                    if gi == 7:
                        try_loads()
                if i >= 1:
                    ns = nsub(i - 1)
                    if 0 <= gi - 3 < ns:
                        emit_O_gate(i - 1, gi - 3)
                        deferred.append((i - 1, gi - 3))
                    if 0 <= gi - 2 < ns:
                        emit_O_tr2(i - 1, gi - 2)
                    if gi < ns:
                        emit_O_wout(i - 1, gi)
                if i + 1 < NT:
                    nn = nsub(i + 1)
                    for (jn, g_norm, g_tr) in ((0, 0, 1), (1, 1, 2), (2, 4, 5), (3, 5, 6)):
                        if jn < nn:
                            if gi == g_tr:
                                emit_N_tr(i + 1, jn)
                    for (jn, g_norm, g_tr) in ((0, 0, 1), (1, 1, 2), (2, 4, 5), (3, 5, 6)):
                        if jn < nn:
                            if gi == g_norm:
                                emit_N_norm(i + 1, jn)
                for (di, dj) in deferred:
                    emit_O_fin(di, dj)
                if tile is None:
                    tick()
        while chains:
            tick()

        S.finalize()
        final_waits = [(n, S.dma_count[n]) for n in dma_names if n.startswith("sty") or n.startswith("state")
                       if S.dma_count.get(n, 0) > 0]
        with nc.Block() as block:
            @block.tensor
            def _(e):
                S.emit("pe", e, sems, dma_sems)

            @block.scalar
            def _(e):
                S.emit("act", e, sems, dma_sems)

            @block.vector
            def _(e):
                S.emit("dve", e, sems, dma_sems)

            @block.gpsimd
            def _(e):
                S.emit("pool", e, sems, dma_sems)

            @block.sync
            def _(e):
                S.emit("sp", e, sems, dma_sems)
                for n, v in final_waits:
                    e.wait_ge(dma_sems[n], v)
    return nc


def _prep_inputs(inp, NPS):
    f = lambda a: np.ascontiguousarray(np.asarray(a, dtype=np.float32))
    x_prompt, x_sample = f(inp["x_prompt"]), f(inp["x_sample"])
    p_prompt, p_sample = f(inp["p_prompt"])[0], f(inp["p_sample"])[0]
    cache_conv, cache_pool = f(inp["cache_conv"])[0], f(inp["cache_pool"])[0]
    NPTOK = NPS * 512
    B, SEQ, _ = x_prompt.shape
    cps = NCORES // B
    assert SEQ == cps * NPTOK
    cols = []
    cols += [f(inp["g_mix"])[0].reshape(8, 128)]
    cols += [f(inp["g_ple"])[0].reshape(8, 128)]
    cols += [f(inp["conv_w"])[0].reshape(12, 128)]
    cols += [f(inp["conv_b"])[0].reshape(4, 128)]
    cols += [f(inp["pool_scale"])[0].reshape(4, 128)]
    prm = np.ascontiguousarray(np.concatenate(cols, axis=0).T)
    gfin = np.ascontiguousarray(np.broadcast_to(f(inp["g_final"])[None, :], (128, D)))
    import ml_dtypes
    ident = np.eye(128, dtype=np.float32).astype(ml_dtypes.bfloat16)
    icnt_first = np.zeros((128, 4, 16), np.float32)
    icnt_rest = np.zeros((128, 4, 16), np.float32)
    for g in range(4):
        w = 2 ** (g + 1)
        icnt_first[:, g, :] = 1.0 / np.minimum(np.arange(16) + 1, w)
        icnt_rest[:, g, :] = 1.0 / w
    w_in_r = np.ascontiguousarray(f(inp["w_in"])[0].reshape(8, 128, 24, 128).transpose(2, 1, 0, 3).reshape(24, 128, D))
    shared = dict(w_in=w_in_r, w_out=f(inp["w_out"])[0], w_gate=f(inp["w_ple_gate"])[0],
                  w_ple=f(inp["w_ple"])[0], pool_w=f(inp["pool_w"])[0], prm=prm, gfin=gfin, ident=ident)
    maps = []
    for c in range(NCORES):
        b, ch = divmod(c, cps)
        s0 = ch * NPTOK
        xp = np.zeros((HALO + NPTOK, D), np.float32)
        xp[HALO:] = x_prompt[b, s0:s0 + NPTOK]
        if ch > 0:
            xp[:HALO] = x_prompt[b, s0 - HALO:s0]
        sl = slice(4 * c, 4 * c + 4)
        cc = cache_conv[sl].reshape(4, 2, 4, 128)
        cp = cache_pool[sl].reshape(4, 15, 4, 128)
        m = dict(shared)
        m.update(xp=xp, pp=np.ascontiguousarray(p_prompt[b, s0:s0 + NPTOK]),
                 xs=np.ascontiguousarray(x_sample[sl].reshape(128, D)),
                 ps=np.ascontiguousarray(p_sample[sl].reshape(128, PLE)),
                 cconvT=np.ascontiguousarray(cc.transpose(3, 2, 0, 1)),
                 cpoolT=np.ascontiguousarray(cp.transpose(3, 2, 0, 1)),
                 icnt=icnt_first if ch == 0 else icnt_rest)
        maps.append(m)
    return maps, (B, SEQ, cps)


def _assemble(results, meta, NPS):
    B, SEQ, cps = meta
    NPTOK = NPS * 512
    y_prompt = np.empty((B, SEQ, D), np.float32)
    y_sample = np.empty((32, 32, D), np.float32)
    sc_p = np.empty((1, B, 2, 512), np.float32)
    sp_p = np.empty((1, B, 15, 512), np.float32)
    sc_s = np.empty((1, 32, 2, 512), np.float32)
    sp_s = np.empty((1, 32, 15, 512), np.float32)
    for c, r in enumerate(results):
        b, ch = divmod(c, cps)
        y_prompt[b, ch * NPTOK:(ch + 1) * NPTOK] = r["yp"]
        y_sample[4 * c:4 * c + 4] = r["ys"].reshape(4, 32, D)
        sc_s[0, 4 * c:4 * c + 4] = r["sconv_s"].transpose(2, 3, 1, 0).reshape(4, 2, 512)
        sp_s[0, 4 * c:4 * c + 4] = r["spool_s"].transpose(2, 3, 1, 0).reshape(4, 15, 512)
        if ch == cps - 1:
            sc_p[0, b] = r["sconv_p"].transpose(2, 1, 0).reshape(2, 512)
            sp_p[0, b] = r["spool_p"].transpose(2, 1, 0).reshape(15, 512)
    return (y_prompt, y_sample, sc_p, sp_p, sc_s, sp_s)


_NC_CACHE = {}


def _run(inp, NPS):
    maps, meta = _prep_inputs(inp, NPS)
    if NPS not in _NC_CACHE:
        _NC_CACHE[NPS] = build(NPS)
    res = run_bass_kernel_spmd(_NC_CACHE[NPS], maps, core_ids=list(range(NCORES)))
    return _assemble(res.results, meta, NPS)


def kernel(**inputs):
    return _run(inputs, 8)
```

```python
import numpy as np
from contextlib import ExitStack
import concourse.bass as bass
import concourse.mybir as mybir
from concourse.bass_utils import run_bass_kernel_spmd

F32 = mybir.dt.float32
BF16 = mybir.dt.bfloat16
AF = mybir.ActivationFunctionType
ALU = mybir.AluOpType

D = 1024
PW = 3072
PLE = 256
EPS = 1e-6
HALO = 16
NCORES = 8


class Sched:
    ENGS = ("pe", "act", "dve", "pool", "sp")

    def __init__(self):
        self.ops = {e: [] for e in self.ENGS}
        self.last_writer = {}
        self.readers = {}
        self.prev_access = {}
        self.dma_count = {}
        self.total_wait = set()

    def op(self, eng, fn, reads=(), writes=(), dma=None):
        idx = len(self.ops[eng])
        deps = {}

        def add(d, raw):
            if d is None:
                return
            deps[d] = deps.get(d, False) or raw

        for k in reads:
            if isinstance(k, str) and k.startswith("PS:"):
                pa = self.prev_access.get(k)
                if pa is not None:
                    add((pa[0], pa[1]), pa[2])
            else:
                add(self.last_writer.get(k), True)
        for k in writes:
            if isinstance(k, str) and k.startswith("PS:"):
                pa = self.prev_access.get(k)
                if pa is not None:
                    add((pa[0], pa[1]), False)
            else:
                add(self.last_writer.get(k), False)
                for r in self.readers.get(k, ()):
                    add(r, False)
        rec = dict(fn=fn, deps=deps, dma=dma, signal=False, dma_val=None)
        if dma is not None:
            self.dma_count[dma] = self.dma_count.get(dma, 0) + 16
            rec["dma_val"] = self.dma_count[dma]
        self.ops[eng].append(rec)
        for k in reads:
            if isinstance(k, str) and k.startswith("PS:"):
                self.prev_access[k] = (eng, idx, False)
            else:
                self.readers.setdefault(k, []).append((eng, idx))
        for k in writes:
            if isinstance(k, str) and k.startswith("PS:"):
                self.prev_access[k] = (eng, idx, True)
            else:
                self.last_writer[k] = (eng, idx)
                self.readers[k] = []
        return (eng, idx)

    def finalize(self):
        for eng in self.ENGS:
            for rec in self.ops[eng]:
                keep = []
                for (de, di), raw in rec["deps"].items():
                    drec = self.ops[de][di]
                    if de == eng and drec["dma"] is None:
                        if eng == "pe":
                            continue
                    keep.append((de, di))
                    if drec["dma"] is None:
                        drec["signal"] = True
                rec["keep"] = keep
        for eng in self.ENGS:
            c = 0
            for rec in self.ops[eng]:
                if rec["dma"] is None and rec["signal"]:
                    c += 1
                    rec["sig_val"] = c

    def emit(self, eng, engine, sems, dma_sems):
        known = {}
        for rec in self.ops[eng]:
            waits = {}
            for (de, di) in rec["keep"]:
                drec = self.ops[de][di]
                if drec["dma"] is not None:
                    key = ("dma", drec["dma"])
                    val = drec["dma_val"]
                    if drec["dma"] in self.total_wait:
                        val = self.dma_count[drec["dma"]]
                else:
                    key = ("eng", de)
                    val = drec["sig_val"]
                if val > waits.get(key, 0):
                    waits[key] = val
            for key, val in waits.items():
                if known.get(key, 0) >= val:
                    continue
                known[key] = val
                sem = dma_sems[key[1]] if key[0] == "dma" else sems[key[1]]
                engine.wait_ge(sem, val)
            ins = rec["fn"](engine)
            if rec["dma"] is not None:
                ins.then_inc(dma_sems[rec["dma"]], 16)
            elif rec["signal"]:
                ins.then_inc(sems[eng], 1)


def build(NPS):
    NT = NPS + 1
    NPTOK = NPS * 512
    nc = bass.Bass("TRN2", target_bir_lowering=False)

    def din(name, shape):
        return nc.dram_tensor(name, shape, F32, kind="ExternalInput").ap()

    def dout(name, shape):
        return nc.dram_tensor(name, shape, F32, kind="ExternalOutput").ap()

    xp = din("xp", [HALO + NPTOK, D])
    pp = din("pp", [NPTOK, PLE])
    xs = din("xs", [128, D])
    ps_in = din("ps", [128, PLE])
    cconvT = din("cconvT", [128, 4, 4, 2])
    cpoolT = din("cpoolT", [128, 4, 4, 15])
    w_in = din("w_in", [24, 128, D])
    w_out = din("w_out", [D, D])
    w_gate = din("w_gate", [D, D])
    w_ple = din("w_ple", [PLE, D])
    pool_w = din("pool_w", [4, 128, 128])
    prm_in = din("prm", [128, 36])
    gfin_in = din("gfin", [128, D])
    icnt_in = din("icnt", [128, 4, 16])
    ident_in = nc.dram_tensor("ident", [128, 128], BF16, kind="ExternalInput").ap()

    yp = dout("yp", [NPTOK, D])
    ys = dout("ys", [128, D])
    sconv_p = dout("sconv_p", [128, 4, 2])
    spool_p = dout("spool_p", [128, 4, 15])
    sconv_s = dout("sconv_s", [128, 4, 4, 2])
    spool_s = dout("spool_s", [128, 4, 4, 15])

    S = Sched()
    es = ExitStack()
    with es:
        def sb(name, shape, dt):
            return es.enter_context(nc.sbuf_tensor(name, shape, dt))

        def psum(name, shape, dt):
            return es.enter_context(nc.psum_tensor(name, shape, dt))

        NXB = 10
        NPB = 1
        xh = sb("xh", [128, NXB, D], F32)
        pin = sb("pin", [128, NPB, PLE], F32)
        winb = sb("winb", [128, 8, PW], BF16)
        woutb = sb("woutb", [128, 8, D], BF16)
        wgb = sb("wgb", [128, 8, D], BF16)
        wpb = sb("wpb", [128, 2, D], BF16)
        pwb = sb("pwb", [128, 4, 128], BF16)
        prm = sb("prm_sb", [128, 36], F32)
        gfin = sb("gfin_sb", [128, D], F32)
        icnt = sb("icnt_sb", [128, 4, 16], F32)
        idb = sb("idb", [128, 128], BF16)
        nh = sb("nh", [128, 1], F32)
        NHN = 3
        NPT = 3
        hn = [sb(f"hn{i}", [128, D], BF16) for i in range(NHN)]
        pb = [sb(f"pb{i}", [128, PLE], BF16) for i in range(1)]
        hnT = [sb(f"hnT{i}", [128, 8, 512], BF16) for i in range(2)]
        hnTh = sb("hnTh", [128, 8, HALO], BF16)
        yT = [sb(f"yT{i}", [128, 8, 512], BF16) for i in range(2)]
        hn2T = [sb(f"hn2T{i}", [128, 8, 128], BF16) for i in range(2)]
        pT = [sb(f"pT{i}", [128, 2, 128], BF16) for i in range(NPT)]
        th = [sb(f"th{i}", [128, D], F32) for i in range(1)]
        xhalo = th[0][0:HALO, :]
        UP = [sb(f"UP{c}", [128, 1, 2 + 512], F32) for c in range(4)]
        VP = [sb(f"VP{g}", [128, 1, 16 + 512], F32) for g in range(4)]
        US = [UP[c][:, 0, 0:4 * 34].rearrange("p (s t) -> p s t", s=4) for c in range(4)]
        VS = [VP[g][:, 0, 0:4 * 48].rearrange("p (s t) -> p s t", s=4) for g in range(4)]
        NPAR = 1
        tA2 = [sb(f"tA2_{i}", [128, 512], F32) for i in range(NPAR)]
        tA3 = [sb(f"tA3_{i}", [128, 512], F32) for i in range(NPAR)]
        tB1 = [sb(f"tB1_{i}", [128, 512], F32) for i in range(NPAR)]
        sA = [sb(f"sA_{i}", [128, 16 + 512], F32) for i in range(NPAR)]
        sB = [sb(f"sB_{i}", [128, 16 + 512], F32) for i in range(NPAR)]
        tB3 = [sb(f"tB3_{i}", [128, 512], BF16) for i in range(NPAR)]
        t16 = sb("t16", [128, 16], F32)
        t16h = sb("t16h", [128, 16], F32)
        icntp = sb("icntp", [128, 16], F32)
        NST = 3 * (NT * 4 + 2) + 8
        st = sb("stat", [128, NST], F32)

        TR = psum("TR", [128, 1024], BF16)
        NF, NTB = 4, 3
        FB = [psum(f"FB{i}", [128, 512], F32) for i in range(NF)]
        TB = [psum(f"TB{i}", [128, 512], F32) for i in range(NTB)]

        sems = {e: es.enter_context(nc.semaphore("s_" + e)) for e in Sched.ENGS}
        dma_names = ([f"ldx{i}" for i in range(NXB)] + [f"ldp{i}" for i in range(NPB)]
                     + [f"sty{i}" for i in range(NXB)] + [f"stg{i}" for i in range(4)]
                     + ["misc", "halo", "statep", "states", "cache", "gfin", "wo"])
        dma_sems = {n: es.enter_context(nc.semaphore("d_" + n)) for n in dma_names}
        S.total_wait.add("misc")
        S.total_wait.add("cache")
        S.total_wait.add("wo")
        S.total_wait.add("statep")
        S.total_wait.add("states")

        ctr = dict(stat=0, f=0, t=0, hn=0, par=0)

        def stat_alloc(n):
            c = ctr["stat"]
            ctr["stat"] += n
            assert ctr["stat"] <= NST
            return c

        def falloc():
            b = ctr["f"] % NF
            ctr["f"] += 1
            return b

        def talloc():
            b = ctr["t"] % NTB
            ctr["t"] += 1
            return b

        class Ring:
            def __init__(self, n, name):
                self.free = list(range(n))
                self.name = name

            def alloc(self):
                assert self.free, f"ring {self.name} exhausted"
                return self.free.pop(0)

            def release(self, s):
                assert s not in self.free
                self.free.append(s)

        hn_ring = Ring(NHN, "hn")
        pT_ring = Ring(NPT, "pT")
        hn2T_ring = Ring(2, "hn2T")

        def hnalloc():
            return hn_ring.alloc()

        S.op("sp", lambda e: e.dma_start(out=prm[:], in_=prm_in), writes=["prm"], dma="misc")
        S.op("sp", lambda e: e.dma_start(out=idb[:], in_=ident_in), writes=["idb"], dma="misc")
        S.op("sp", lambda e: e.dma_start(out=xhalo, in_=xp[0:HALO, :]), writes=[("th", 0, 0), ("th", 0, 1)], dma="halo")

        slot_of = {}
        ldstate = dict(next=0, stores=0)

        def xslot(i, j):
            return slot_of[(i, j)]

        def nsub(i):
            return 4 if i < NPS else 1

        olist = [(i, j) for i in range(NT) for j in range(nsub(i))]
        oindex = {ij: q for q, ij in enumerate(olist)}

        def emit_load(q):
            i, j = olist[q]
            slot = q % NXB
            slot_of[(i, j)] = slot
            if i < NPS:
                r0 = (i * 4 + j) * 128
                xsrc = xp[HALO + r0:HALO + r0 + 128, :]
            else:
                xsrc = xs
            S.op("sp", lambda e: e.dma_start(out=xh[:, slot, :], in_=xsrc), writes=[("xh", slot)], dma=f"ldx{slot}")

        def emit_pload(q):
            if q >= len(olist):
                return
            i, j = olist[q]
            ps_ = q % NPB
            if i < NPS:
                r0 = (i * 4 + j) * 128
                psrc = pp[r0:r0 + 128, :]
            else:
                psrc = ps_in
            S.op("sp", lambda e: e.dma_start(out=pin[:, ps_, :], in_=psrc), writes=[("pin", ps_)], dma=f"ldp{ps_}")

        def try_loads(limit=None):
            while ldstate["next"] < len(olist) and ldstate["next"] < ldstate["stores"] + NXB:
                if limit is not None and ldstate["next"] >= limit:
                    break
                emit_load(ldstate["next"])
                ldstate["next"] += 1

        try_loads(limit=4)
        S.op("sp", lambda e: e.dma_start(out=icnt[:], in_=icnt_in), writes=["icnt"], dma="misc")
        emit_pload(0)

        def emit_cache_loads():
            for c in range(4):
                S.op("sp", lambda e, c=c: e.dma_start(out=US[c][:, :, 0:2], in_=cconvT[:, c, :, :]),
                     writes=[("U", c)], dma="cache")
                S.op("sp", lambda e, c=c: e.dma_start(out=VS[c][:, :, 1:16], in_=cpoolT[:, c, :, :]),
                     writes=[("V", c)], dma="cache")
        S.op("pool", lambda e: e.memset(nh[:], -0.5), writes=["nh"])

        yT1f = yT[1][:].bitcast(F32).rearrange("p k n -> p (k n)")
        stg_views = [xh[:, NXB - 1, :], th[0][:, :], yT1f[:, 0:1024], yT1f[:, 1024:2048]]
        stg_keys = [[("xh", NXB - 1)], [("th", 0, 0), ("th", 0, 1)],
                    [(("yT", 1), c) for c in range(4)], [(("yT", 1), c) for c in range(4, 8)]]
        stg_ctr = [0]

        def stage(src_ap, shape3, consume):
            i = stg_ctr[0] % 4
            stg_ctr[0] += 1
            v = stg_views[i]
            if shape3 is not None:
                ncol = shape3[0] * shape3[1]
                v = v[:, 0:ncol].rearrange("p (a n) -> p a n", a=shape3[0])
            S.op("sp", lambda e: e.dma_start(out=v, in_=src_ap), writes=stg_keys[i], dma=f"stg{i}")
            consume(v, stg_keys[i])

        chunk_groups = []
        for c in range(4):
            chunk_groups.append([16 + c, 20 + c])
            chunk_groups.append([c, 8 + c, 12 + c, 4 + c])
        gmix_b = prm[:, 0:8].unsqueeze(2).to_broadcast([128, 8, 128])
        win_ctr = [0]

        def emit_win_group(gidx):
            if gidx >= len(chunk_groups):
                return
            for cc in chunk_groups[gidx]:
                eng = "dve"
                win_ctr[0] += 1

                def consume(v, keys, cc=cc, eng=eng):
                    S.op(eng, lambda e: e.tensor_tensor(out=winb[:, :, cc * 128:(cc + 1) * 128], in0=v, in1=gmix_b,
                                                        op=ALU.mult),
                         reads=keys + ["prm"], writes=[("winb", cc)])
                stage(w_in[cc].rearrange("p (k n) -> p k n", k=8), (8, 128), consume)

        def winb_key(k, c):
            return ("winb", c)

        def emit_wout_stage(k):
            S.op("pool", lambda e: e.dma_start(out=woutb[:, k, :], in_=w_out[k * 128:(k + 1) * 128, :]),
                 writes=[("woutb", k)], dma="wo")

        def emit_wgate_stage(k):
            def consume(v, keys):
                S.op("act", lambda e: e.activation(out=wgb[:, k, :], in_=v, func=AF.Copy, scale=prm[:, 8 + k:9 + k]),
                     reads=keys + ["prm"], writes=[("wgb", k)])
            stage(w_gate[k * 128:(k + 1) * 128, :], None, consume)

        def emit_wple_stage(k):
            def consume(v, keys):
                S.op("act", lambda e: e.activation(out=wpb[:, k, :], in_=v, func=AF.Copy, scale=0.5),
                     reads=keys, writes=[("wpb", k)])
            stage(w_ple[k * 128:(k + 1) * 128, :], None, consume)

        def emit_poolw_stage():
            def consume(v, keys):
                S.op("act", lambda e: e.activation(out=pwb[:], in_=v, func=AF.Copy), reads=keys, writes=["pwb"])
            stage(pool_w.rearrange("g c d -> c g d"), (4, 128), consume)

        chains = []

        def add_chain(stages):
            ch = list(stages)
            ch.pop(0)()
            if ch:
                chains.append(ch)
            return ch

        def tick():
            for ch in list(chains):
                ch.pop(0)()
                if not ch:
                    chains.remove(ch)

        def finish(ch):
            while ch:
                ch.pop(0)()
            if ch in chains:
                chains.remove(ch)

        def emit_norm(x_ap, ntok, xkeys):
            c = stat_alloc(1)
            h = hnalloc()
            ss = st[0:ntok, c:c + 1]
            ch = add_chain([
                lambda: S.op("act", lambda e: e.activation(out=hn[h][0:ntok, :], in_=x_ap, func=AF.Square, accum_out=ss),
                             reads=xkeys, writes=[("st", c), ("hn", h)]),
                lambda: (S.op("dve", lambda e: e.tensor_scalar(out=ss, in0=ss, scalar1=1.0 / D, scalar2=EPS, op0=ALU.mult,
                                                               op1=ALU.add), reads=[("st", c)], writes=[("st", c)]),
                         S.op("pool", lambda e: e.tensor_tensor(out=ss, in0=ss, in1=nh[0:ntok, :], op=ALU.pow),
                              reads=[("st", c), "nh"], writes=[("st", c)])),
                lambda: S.op("act", lambda e: e.activation(out=hn[h][0:ntok, :], in_=x_ap, func=AF.Copy, scale=ss),
                             reads=xkeys + [("st", c)], writes=[("hn", h)]),
            ])
            return h, ch

        def emit_transposes(src_tile, nch, ntok, src_keys, dst_ap, dst_keys):
            for k in range(nch):
                S.op("pe", lambda e, k=k: e.transpose(out=TR[:, k * 128:k * 128 + ntok],
                                                     in_=src_tile[0:ntok, k * 128:(k + 1) * 128],
                                                     identity=idb[0:ntok, 0:ntok]),
                     reads=src_keys + ["idb"], writes=["PS:TR"])
            src = TR[:, 0:nch * 128].rearrange("p (k t) -> p k t", k=nch)[:, :, 0:ntok]
            S.op("dve", lambda e: e.tensor_copy(out=dst_ap, in_=src), reads=["PS:TR"], writes=dst_keys)

        def proj_chunk(c, src_ap, src_keys, n):
            b = falloc()
            for k in range(8):
                S.op("pe", lambda e, k=k: e.matmul(FB[b][:, 0:n], lhsT=winb[:, k, c * 128:(c + 1) * 128],
                                                   rhs=src_ap[:, k, :], start=(k == 0), stop=(k == 7)),
                     reads=src_keys + [winb_key(k, c)], writes=[f"PS:F{b}"])
            tick()
            return b

        def cw(jj, c):
            col = 16 + jj * 4 + c
            return prm[:, col:col + 1]

        def cbias(c):
            return prm[:, 28 + c:29 + c]

        def pscale(g):
            return prm[:, 32 + g:33 + g]

        def mixer_A(c, tile):
            S_, T_, n = tile["S"], tile["T"], tile["n"]
            U = tile["U"][c]
            ukey = ("U", c)
            src, skeys = tile["hnT"], tile["hnT_keys"]
            par = 0

            def v3(ap):
                return ap.rearrange("p (s t) -> p s t", s=S_)

            bh = proj_chunk(c, src, skeys, n)
            S.op("act", lambda e: e.activation(out=U[:, :, 2:2 + T_], in_=v3(FB[bh][:, 0:n]), func=AF.Copy),
                 reads=[f"PS:F{bh}"], writes=[ukey])
            bc = proj_chunk(8 + c, src, skeys, n)
            S.op("dve", lambda e: e.tensor_tensor(out=U[:, :, 2:2 + T_], in0=v3(FB[bc][:, 0:n]),
                                                  in1=U[:, :, 2:2 + T_], op=ALU.mult),
                 reads=[f"PS:F{bc}", ukey], writes=[ukey])
            bz = proj_chunk(12 + c, src, skeys, n)
            S.op("act", lambda e: e.activation(out=tA2[par][:, 0:n], in_=FB[bz][:, 0:n], func=AF.Silu),
                 reads=[f"PS:F{bz}"], writes=[("tA2", par)])
            t3 = v3(tA3[par][:, 0:n])
            S.op("act", lambda e: e.activation(out=t3, in_=U[:, :, 0:T_], func=AF.Identity, scale=cw(0, c), bias=cbias(c)),
                 reads=[ukey, "prm"], writes=[("tA3", par)])
            S.op("dve", lambda e: e.scalar_tensor_tensor(out=t3, in0=U[:, :, 1:1 + T_], scalar=cw(1, c), in1=t3,
                                                         op0=ALU.mult, op1=ALU.add),
                 reads=[ukey, ("tA3", par), "prm"], writes=[("tA3", par)])
            S.op("dve", lambda e: e.scalar_tensor_tensor(out=t3, in0=U[:, :, 2:2 + T_], scalar=cw(2, c), in1=t3,
                                                         op0=ALU.mult, op1=ALU.add),
                 reads=[ukey, ("tA3", par), "prm"], writes=[("tA3", par)])
            bb = proj_chunk(4 + c, src, skeys, n)
            S.op("dve", lambda e: e.tensor_tensor(out=tA3[par][:, 0:n], in0=FB[bb][:, 0:n], in1=tA3[par][:, 0:n],
                                                  op=ALU.mult),
                 reads=[f"PS:F{bb}", ("tA3", par)], writes=[("tA3", par)])
            S.op("pool", lambda e: e.tensor_tensor(out=tile["yT"][:, c, :], in0=tA3[par][:, 0:n], in1=tA2[par][:, 0:n],
                                                   op=ALU.mult),
                 reads=[("tA3", par), ("tA2", par)], writes=[(tile["yT_key"], c)])
            if tile["kind"] == "p":
                if tile["last"]:
                    S.op("sp", lambda e: e.dma_start(out=sconv_p[:, c, :], in_=U[:, 0, T_:T_ + 2]),
                         reads=[ukey], dma="statep")
                else:
                    S.op("pool", lambda e: e.tensor_copy(out=U[:, :, 0:2], in_=U[:, :, T_:T_ + 2]),
                         reads=[ukey], writes=[ukey])
            else:
                S.op("sp", lambda e: e.dma_start(out=sconv_s[:, c, :, :], in_=U[:, :, T_:T_ + 2]),
                     reads=[ukey], dma="states")

        def mixer_B_front(g, tile):
            S_, T_, n = tile["S"], tile["T"], tile["n"]
            V = tile["V"][g]
            vkey = ("V", g)
            src, skeys = tile["hnT"], tile["hnT_keys"]
            par = 0
            W = 2 ** (g + 1)

            def v3(ap):
                return ap.rearrange("p (s t) -> p s t", s=S_)

            def sv(buf):
                return buf[:, 0:S_ * (16 + T_)].rearrange("p (s t) -> p s t", s=S_)

            bv = proj_chunk(16 + g, src, skeys, n)
            S.op("act", lambda e: e.activation(out=V[:, :, 16:16 + T_], in_=v3(FB[bv][:, 0:n]), func=AF.Copy),
                 reads=[f"PS:F{bv}"], writes=[vkey])
            bz = proj_chunk(20 + g, src, skeys, n)
            S.op("act", lambda e: e.activation(out=tB1[par][:, 0:n], in_=FB[bz][:, 0:n], func=AF.Silu),
                 reads=[f"PS:F{bz}"], writes=[("tB1", par)])
            a3, b3 = sv(sA[par]), sv(sB[par])
            aeng = "dve" if g < 3 else "pool"
            stages = []
            lo = -(W - 2)
            stages.append(lambda lo=lo: S.op(aeng, lambda e: e.tensor_tensor(
                out=a3[:, :, 16 + lo:16 + T_], in0=V[:, :, 16 + lo:16 + T_], in1=V[:, :, 15 + lo:15 + T_], op=ALU.add),
                reads=[vkey], writes=[("sA", par)]))
            fin, fkey = a3, ("sA", par)
            if W >= 4:
                lo = -(W - 4)
                stages.append(lambda lo=lo: S.op(aeng, lambda e: e.tensor_tensor(
                    out=b3[:, :, 16 + lo:16 + T_], in0=a3[:, :, 16 + lo:16 + T_], in1=a3[:, :, 14 + lo:14 + T_], op=ALU.add),
                    reads=[("sA", par)], writes=[("sB", par)]))
                fin, fkey = b3, ("sB", par)
            if W >= 8:
                lo = -(W - 8)
                stages.append(lambda lo=lo: S.op(aeng, lambda e: e.tensor_tensor(
                    out=a3[:, :, 16 + lo:16 + T_], in0=b3[:, :, 16 + lo:16 + T_], in1=b3[:, :, 12 + lo:12 + T_], op=ALU.add),
                    reads=[("sB", par)], writes=[("sA", par)]))
                fin, fkey = a3, ("sA", par)
            if W >= 16:
                stages.append(lambda: S.op(aeng, lambda e: e.tensor_tensor(
                    out=b3[:, :, 16:16 + T_], in0=a3[:, :, 16:16 + T_], in1=a3[:, :, 8:8 + T_], op=ALU.add),
                    reads=[("sA", par)], writes=[("sB", par)]))
                fin, fkey = b3, ("sB", par)

            def s_pooled(fin=fin, fkey=fkey):
                if aeng == "dve":
                    S.op("dve", lambda e: e.scalar_tensor_tensor(out=v3(tB3[par][:, 0:n]), in0=fin[:, :, 16:16 + T_],
                                                                 scalar=1.0 / W, in1=V[:, :, 16:16 + T_],
                                                                 op0=ALU.mult, op1=ALU.subtract),
                         reads=[fkey, vkey], writes=[("tB3", par)])
                else:
                    S.op("pool", lambda e: e.tensor_scalar(out=fin[:, :, 16:16 + T_], in0=fin[:, :, 16:16 + T_],
                                                           scalar1=1.0 / W, scalar2=0.0, op0=ALU.mult, op1=ALU.add),
                         reads=[fkey], writes=[fkey])
                    S.op("pool", lambda e: e.tensor_tensor(out=v3(tB3[par][:, 0:n]), in0=fin[:, :, 16:16 + T_],
                                                           in1=V[:, :, 16:16 + T_], op=ALU.subtract),
                         reads=[fkey, vkey], writes=[("tB3", par)])
                if tile["first"]:
                    S.op(aeng, lambda e: e.tensor_tensor(out=t16[:], in0=fin[:, 0, 16:32],
                                                         in1=(icnt[:, g, :] if aeng == "dve" else icntp[:, :]), op=ALU.mult),
                         reads=[fkey, "icnt"], writes=["t16"])
                    S.op(aeng, lambda e: e.tensor_tensor(out=tB3[par][:, 0:16], in0=t16[:], in1=V[:, 0, 16:32],
                                                         op=ALU.subtract),
                         reads=["t16", vkey], writes=[("tB3", par)])
            adds = stages
            stages = []
            for a_ in range(0, len(adds), 2):
                grp = adds[a_:a_ + 2]
                stages.append(lambda grp=grp: [f_() for f_ in grp])
            stages.append(s_pooled)
            def s_hist():
                if tile["kind"] == "p":
                    if tile["last"]:
                        S.op("sp", lambda e: e.dma_start(out=spool_p[:, g, :], in_=V[:, 0, T_ + 1:T_ + 16]),
                             reads=[vkey], dma="statep")
                    else:
                        S.op("pool", lambda e: e.tensor_copy(out=V[:, :, 0:16], in_=V[:, :, T_:T_ + 16]),
                             reads=[vkey], writes=[vkey])
                else:
                    S.op("sp", lambda e: e.dma_start(out=spool_s[:, g, :, :], in_=V[:, :, T_ + 1:T_ + 16]),
                         reads=[vkey], dma="states")
            stages.append(s_hist)
            bch = add_chain(stages)
            return par, bch

        def mixer_B_back(g, tile, par_ch):
            par, bch = par_ch
            finish(bch)
            n = tile["n"]
            bm = falloc()
            S.op("pe", lambda e: e.matmul(FB[bm][:, 0:n], lhsT=pwb[:, g, :], rhs=tB3[par][:, 0:n], start=True, stop=True),
                 reads=[("tB3", par), "pwb"], writes=[f"PS:F{bm}"])
            tick()
            S.op("dve", lambda e: e.scalar_tensor_tensor(out=tile["yT"][:, 4 + g, :], in0=FB[bm][:, 0:n], scalar=pscale(g),
                                                         in1=tB1[par][:, 0:n], op0=ALU.mult, op1=ALU.mult),
                 reads=[f"PS:F{bm}", ("tB1", par), "prm"], writes=[(tile["yT_key"], 4 + g)])

        def emit_halo_front():
            h, ch = emit_norm(xhalo, HALO, [("th", 0, 0), ("th", 0, 1)])
            finish(ch)
            emit_transposes(hn[h], 8, HALO, [("hn", h)], hnTh[:], ["hnTh"])
            hn_ring.release(h)

        def emit_halo_A(c):
            bh = proj_chunk(c, hnTh, ["hnTh"], HALO)
            S.op("act", lambda e: e.activation(out=t16h[:, 0:HALO], in_=FB[bh][:, 0:HALO], func=AF.Copy),
                 reads=[f"PS:F{bh}"], writes=["t16h"])
            bc = proj_chunk(8 + c, hnTh, ["hnTh"], HALO)
            S.op("dve", lambda e: e.tensor_tensor(out=UP[c][:, 0, 0:2], in0=FB[bc][:, HALO - 2:HALO],
                                                  in1=t16h[:, HALO - 2:HALO], op=ALU.mult),
                 reads=[f"PS:F{bc}", "t16h"], writes=[("U", c)])

        def emit_halo_B(g):
            bv = proj_chunk(16 + g, hnTh, ["hnTh"], HALO)
            S.op("act", lambda e: e.activation(out=VP[g][:, 0, 0:16], in_=FB[bv][:, 0:HALO], func=AF.Copy),
                 reads=[f"PS:F{bv}"], writes=[("V", g)])

        def tile_desc(i):
            par = i % 2
            if i < NPS:
                return dict(kind="p", S=1, T=512, n=512, U=UP, V=VP, hnT=hnT[par][:, :, :], hnT_keys=[("hnT", par)],
                            yT=yT[par][:, :, :], yT_key=("yT", par), first=(i == 0), last=(i == NPS - 1), idx=i)
            return dict(kind="s", S=4, T=32, n=128, U=US, V=VS, hnT=hnT[par][:, :, 0:128], hnT_keys=[("hnT", par)],
                        yT=yT[par][:, :, 0:128], yT_key=("yT", par), first=False, last=False, idx=i)

        nstate = {}

        def emit_N_norm(i, j):
            while (i, j) not in slot_of and chains:
                tick()
            slot = xslot(i, j)
            nstate[(i, j)] = emit_norm(xh[:, slot, :], 128, [("xh", slot)])

        def emit_N_tr(i, j):
            par = i % 2
            h, ch = nstate[(i, j)]
            finish(ch)
            emit_transposes(hn[h], 8, 128, [("hn", h)], hnT[par][:, :, j * 128:(j + 1) * 128], [("hnT", par)])
            hn_ring.release(h)

        def emit_N(i, j):
            emit_N_norm(i, j)
            emit_N_tr(i, j)

        ostate = {}

        def emit_pcast(q):
            if q >= len(olist):
                return
            S.op("act", lambda e: e.activation(out=pb[0][:], in_=pin[:, q % NPB, :], func=AF.Copy),
                 reads=[("pin", q % NPB)], writes=[("pb", 0)])

        def emit_O_wout(i, j):
            slot = xslot(i, j)
            par = i % 2
            b0, b1 = talloc(), talloc()
            ykeys = [(("yT", par), c) for c in range(8)]
            for k in range(8):
                for hf, b in ((0, b0), (1, b1)):
                    S.op("pe", lambda e, k=k, hf=hf, b=b: e.matmul(TB[b][:, :], lhsT=yT[par][:, k, j * 128:(j + 1) * 128],
                                                                   rhs=woutb[:, k, hf * 512:(hf + 1) * 512],
                                                                   start=(k == 0), stop=(k == 7)),
                         reads=ykeys + [("woutb", k)], writes=[f"PS:T{b}"])
            tick()
            q_o = oindex[(i, j)]
            pslot = pT_ring.alloc()
            pbslot = 0
            emit_pcast(q_o)
            emit_transposes(pb[pbslot], 2, 128, [("pb", pbslot)], pT[pslot][:], [("pT", pslot)])
            emit_pload(q_o + 1)
            for hf, b in ((0, b0), (1, b1)):
                S.op("dve", lambda e, hf=hf, b=b: e.tensor_tensor(out=xh[:, slot, hf * 512:(hf + 1) * 512], in0=TB[b][:, :],
                                                                  in1=xh[:, slot, hf * 512:(hf + 1) * 512], op=ALU.add),
                     reads=[f"PS:T{b}", ("xh", slot)], writes=[("xh", slot)])
            h, ch = emit_norm(xh[:, slot, :], 128, [("xh", slot)])
            ostate[(i, j)] = dict(h=h, pslot=pslot, ch=ch)

        def emit_O_tr2(i, j):
            o = ostate[(i, j)]
            finish(o["ch"])
            q = hn2T_ring.alloc()
            emit_transposes(hn[o["h"]], 8, 128, [("hn", o["h"])], hn2T[q][:], [("hn2T", q)])
            hn_ring.release(o["h"])
            o["q"] = q

        def emit_O_gate(i, j):
            o = ostate[(i, j)]
            slot = xslot(i, j)
            q, pslot = o["q"], o["pslot"]
            g0, g1 = talloc(), talloc()
            for k in range(8):
                for hf, b in ((0, g0), (1, g1)):
                    S.op("pe", lambda e, k=k, hf=hf, b=b: e.matmul(TB[b][:, :], lhsT=hn2T[q][:, k, :],
                                                                   rhs=wgb[:, k, hf * 512:(hf + 1) * 512],
                                                                   start=(k == 0), stop=(k == 7)),
                         reads=[("hn2T", q), ("wgb", k)], writes=[f"PS:T{b}"])
            tick()
            tq = 0
            for hf, b in ((0, g0), (1, g1)):
                S.op("act", lambda e, hf=hf, b=b: e.activation(out=th[tq][:, hf * 512:(hf + 1) * 512], in_=TB[b][:, :],
                                                               func=AF.Tanh, scale=0.5),
                     reads=[f"PS:T{b}"], writes=[("th", tq, hf)])
            p0, p1 = talloc(), talloc()
            for k in range(2):
                for hf, b in ((0, p0), (1, p1)):
                    S.op("pe", lambda e, k=k, hf=hf, b=b: e.matmul(TB[b][:, :], lhsT=pT[pslot][:, k, :],
                                                                   rhs=wpb[:, k, hf * 512:(hf + 1) * 512],
                                                                   start=(k == 0), stop=(k == 1)),
                         reads=[("pT", pslot), ("wpb", k)], writes=[f"PS:T{b}"])
            tick()
            for hf, b in ((0, p0), (1, p1)):
                S.op("dve", lambda e, hf=hf, b=b: e.scalar_tensor_tensor(out=th[tq][:, hf * 512:(hf + 1) * 512],
                                                                         in0=th[tq][:, hf * 512:(hf + 1) * 512], scalar=1.0,
                                                                         in1=TB[b][:, :], op0=ALU.add, op1=ALU.mult),
                     reads=[f"PS:T{b}", ("th", tq, hf)], writes=[("th", tq, hf)])
            hn2T_ring.release(q)
            pT_ring.release(pslot)

        def emit_O_fin(i, j):
            slot = xslot(i, j)
            x_ap = xh[:, slot, :]
            xkeys = [("xh", slot)]
            c = stat_alloc(1)
            ss = st[:, c:c + 1]
            if i < NPS:
                r0 = (i * 4 + j) * 128
                dst = yp[r0:r0 + 128, :]
            else:
                dst = ys

            def s00():
                S.op("pool", lambda e: e.tensor_tensor(out=x_ap, in0=x_ap, in1=th[0][:], op=ALU.add),
                     reads=xkeys + [("th", 0, 0), ("th", 0, 1)], writes=xkeys)

            def s0():
                S.op("act", lambda e: e.activation(out=th[0][:, :], in_=x_ap, func=AF.Square, accum_out=ss),
                     reads=xkeys, writes=[("st", c), ("th", 0, 0), ("th", 0, 1)])

            def s1():
                S.op("dve", lambda e: e.tensor_scalar(out=ss, in0=ss, scalar1=1.0 / D, scalar2=EPS, op0=ALU.mult,
                                                      op1=ALU.add), reads=[("st", c)], writes=[("st", c)])

            def s2():
                S.op("pool", lambda e: e.tensor_tensor(out=ss, in0=ss, in1=nh[:, :], op=ALU.pow),
                     reads=[("st", c), "nh"], writes=[("st", c)])

            def s3():
                S.op("dve", lambda e: e.scalar_tensor_tensor(out=x_ap, in0=x_ap, scalar=ss, in1=gfin[:],
                                                             op0=ALU.mult, op1=ALU.mult),
                     reads=xkeys + [("st", c), "gfin"], writes=xkeys)

            def s4():
                S.op("sp", lambda e: e.dma_start(out=dst, in_=x_ap), reads=xkeys, dma=f"sty{slot}")
                ldstate["stores"] += 1
                try_loads()

            def s12():
                s1()
                s2()

            def s34():
                s3()
                s4()

            return add_chain([s00, s0, s12, s34])

        emit_halo_front()
        emit_N_norm(0, 0)
        emit_N_norm(0, 1)
        emit_N_norm(0, 2)
        tick()
        tick()
        emit_N_tr(0, 0)
        emit_N_norm(0, 3)
        tick()
        tick()
        emit_win_group(0)
        emit_N_tr(0, 1)
        emit_N_tr(0, 2)
        emit_win_group(1)
        emit_N_tr(0, 3)
        S.op("pool", lambda e: e.tensor_scalar(out=icntp[:, :], in0=icnt[:, 3, :], scalar1=16.0, scalar2=0.0,
                                               op0=ALU.mult, op1=ALU.add), reads=["icnt"], writes=["icnt"])
        emit_poolw_stage()
        try_loads(limit=6)

        order = [("B", 0), ("A", 0), ("B", 1), ("A", 1), ("B", 2), ("A", 2), ("B", 3), ("A", 3)]
        pendB = None
        for i in range(NT + 1):
            tile = tile_desc(i) if i < NT else None
            if i == NPS:
                while chains:
                    tick()
                emit_cache_loads()
                for gi, (kind, c) in enumerate(order):
                    if kind == "A":
                        mixer_A(c, tile)
                    else:
                        par = mixer_B_front(c, tile)
                    if pendB is not None:
                        mixer_B_back(pendB[0], pendB[2], pendB[1])
                        pendB = None
                    if kind == "B":
                        pendB = (c, par, tile)
                assert pendB is None
                L = [(i - 1, j) for j in range(nsub(i - 1))] + [(i, 0)]
                for s in range(len(L) + 3):
                    deferred = []
                    if 0 <= s - 3 < len(L):
                        emit_O_gate(*L[s - 3])
                        deferred.append(L[s - 3])
                    if 0 <= s - 2 < len(L):
                        emit_O_tr2(*L[s - 2])
                    if s < len(L):
                        emit_O_wout(*L[s])
                    for (di, dj) in deferred:
                        emit_O_fin(di, dj)
                    tick()
                break
            for gi, (kind, c) in enumerate(order):
                deferred = []
                if i == 0:
                    if kind == "A":
                        emit_halo_A(c)
                    else:
                        emit_halo_B(c)
                if tile is not None:
                    if kind == "A":
                        mixer_A(c, tile)
                    else:
                        par = mixer_B_front(c, tile)
                if pendB is not None:
                    mixer_B_back(pendB[0], pendB[2], pendB[1])
                    pendB = None
                if tile is not None and kind == "B":
                    pendB = (c, par, tile)
                if i == 0:
                    emit_win_group(gi + 2)
                    for k in {3: (0, 1, 2, 3), 4: (4, 5, 6, 7)}.get(gi, ()):
                        emit_wout_stage(k)
                    for k in {6: (0, 1, 2, 3), 7: (4, 5, 6, 7)}.get(gi, ()):
                        emit_wgate_stage(k)
                    if gi == 2:
                        try_loads(limit=8)
                    if gi == 7:
                        emit_wple_stage(0)
                        emit_wple_stage(1)
                        S.op("sp", lambda e: e.dma_start(out=gfin[:], in_=gfin_in), writes=["gfin"], dma="gfin")
                    if gi == 7:
                        try_loads()
                if i >= 1:
                    ns = nsub(i - 1)
                    if 0 <= gi - 3 < ns:
                        emit_O_gate(i - 1, gi - 3)
                        deferred.append((i - 1, gi - 3))
                    if 0 <= gi - 2 < ns:
                        emit_O_tr2(i - 1, gi - 2)
                    if gi < ns:
                        emit_O_wout(i - 1, gi)
                if i + 1 < NT:
                    nn = nsub(i + 1)
                    for (jn, g_norm, g_tr) in ((0, 0, 1), (1, 1, 2), (2, 4, 5), (3, 5, 6)):
                        if jn < nn:
                            if gi == g_tr:
                                emit_N_tr(i + 1, jn)
                    for (jn, g_norm, g_tr) in ((0, 0, 1), (1, 1, 2), (2, 4, 5), (3, 5, 6)):
                        if jn < nn:
                            if gi == g_norm:
                                emit_N_norm(i + 1, jn)
                for (di, dj) in deferred:
                    emit_O_fin(di, dj)
                if tile is None:
                    tick()
        while chains:
            tick()

        S.finalize()
        final_waits = [(n, S.dma_count[n]) for n in dma_names if n.startswith("sty") or n.startswith("state")
                       if S.dma_count.get(n, 0) > 0]
        with nc.Block() as block:
            @block.tensor
            def _(e):
                S.emit("pe", e, sems, dma_sems)

            @block.scalar
            def _(e):
                S.emit("act", e, sems, dma_sems)

            @block.vector
            def _(e):
                S.emit("dve", e, sems, dma_sems)

            @block.gpsimd
            def _(e):
                S.emit("pool", e, sems, dma_sems)

            @block.sync
            def _(e):
                S.emit("sp", e, sems, dma_sems)
                for n, v in final_waits:
                    e.wait_ge(dma_sems[n], v)
    return nc


def _prep_inputs(inp, NPS):
    f = lambda a: np.ascontiguousarray(np.asarray(a, dtype=np.float32))
    x_prompt, x_sample = f(inp["x_prompt"]), f(inp["x_sample"])
    p_prompt, p_sample = f(inp["p_prompt"])[0], f(inp["p_sample"])[0]
    cache_conv, cache_pool = f(inp["cache_conv"])[0], f(inp["cache_pool"])[0]
    NPTOK = NPS * 512
    B, SEQ, _ = x_prompt.shape
    cps = NCORES // B
    assert SEQ == cps * NPTOK
    cols = []
    cols += [f(inp["g_mix"])[0].reshape(8, 128)]
    cols += [f(inp["g_ple"])[0].reshape(8, 128)]
    cols += [f(inp["conv_w"])[0].reshape(12, 128)]
    cols += [f(inp["conv_b"])[0].reshape(4, 128)]
    cols += [f(inp["pool_scale"])[0].reshape(4, 128)]
    prm = np.ascontiguousarray(np.concatenate(cols, axis=0).T)
    gfin = np.ascontiguousarray(np.broadcast_to(f(inp["g_final"])[None, :], (128, D)))
    import ml_dtypes
    ident = np.eye(128, dtype=np.float32).astype(ml_dtypes.bfloat16)
    icnt_first = np.zeros((128, 4, 16), np.float32)
    icnt_rest = np.zeros((128, 4, 16), np.float32)
    for g in range(4):
        w = 2 ** (g + 1)
        icnt_first[:, g, :] = 1.0 / np.minimum(np.arange(16) + 1, w)
        icnt_rest[:, g, :] = 1.0 / w
    w_in_r = np.ascontiguousarray(f(inp["w_in"])[0].reshape(8, 128, 24, 128).transpose(2, 1, 0, 3).reshape(24, 128, D))
    shared = dict(w_in=w_in_r, w_out=f(inp["w_out"])[0], w_gate=f(inp["w_ple_gate"])[0],
                  w_ple=f(inp["w_ple"])[0], pool_w=f(inp["pool_w"])[0], prm=prm, gfin=gfin, ident=ident)
    maps = []
    for c in range(NCORES):
        b, ch = divmod(c, cps)
        s0 = ch * NPTOK
        xp = np.zeros((HALO + NPTOK, D), np.float32)
        xp[HALO:] = x_prompt[b, s0:s0 + NPTOK]
        if ch > 0:
            xp[:HALO] = x_prompt[b, s0 - HALO:s0]
        sl = slice(4 * c, 4 * c + 4)
        cc = cache_conv[sl].reshape(4, 2, 4, 128)
        cp = cache_pool[sl].reshape(4, 15, 4, 128)
        m = dict(shared)
        m.update(xp=xp, pp=np.ascontiguousarray(p_prompt[b, s0:s0 + NPTOK]),
                 xs=np.ascontiguousarray(x_sample[sl].reshape(128, D)),
                 ps=np.ascontiguousarray(p_sample[sl].reshape(128, PLE)),
                 cconvT=np.ascontiguousarray(cc.transpose(3, 2, 0, 1)),
                 cpoolT=np.ascontiguousarray(cp.transpose(3, 2, 0, 1)),
                 icnt=icnt_first if ch == 0 else icnt_rest)
        maps.append(m)
    return maps, (B, SEQ, cps)


def _assemble(results, meta, NPS):
    B, SEQ, cps = meta
    NPTOK = NPS * 512
    y_prompt = np.empty((B, SEQ, D), np.float32)
    y_sample = np.empty((32, 32, D), np.float32)
    sc_p = np.empty((1, B, 2, 512), np.float32)
    sp_p = np.empty((1, B, 15, 512), np.float32)
    sc_s = np.empty((1, 32, 2, 512), np.float32)
    sp_s = np.empty((1, 32, 15, 512), np.float32)
    for c, r in enumerate(results):
        b, ch = divmod(c, cps)
        y_prompt[b, ch * NPTOK:(ch + 1) * NPTOK] = r["yp"]
        y_sample[4 * c:4 * c + 4] = r["ys"].reshape(4, 32, D)
        sc_s[0, 4 * c:4 * c + 4] = r["sconv_s"].transpose(2, 3, 1, 0).reshape(4, 2, 512)
        sp_s[0, 4 * c:4 * c + 4] = r["spool_s"].transpose(2, 3, 1, 0).reshape(4, 15, 512)
        if ch == cps - 1:
            sc_p[0, b] = r["sconv_p"].transpose(2, 1, 0).reshape(2, 512)
            sp_p[0, b] = r["spool_p"].transpose(2, 1, 0).reshape(15, 512)
    return (y_prompt, y_sample, sc_p, sp_p, sc_s, sp_s)


_NC_CACHE = {}


def _run(inp, NPS):
    maps, meta = _prep_inputs(inp, NPS)
    if NPS not in _NC_CACHE:
        _NC_CACHE[NPS] = build(NPS)
    res = run_bass_kernel_spmd(_NC_CACHE[NPS], maps, core_ids=list(range(NCORES)))
    return _assemble(res.results, meta, NPS)


def kernel(**inputs):
    return _run(inputs, 8)
```

```python
import numpy as np
from contextlib import ExitStack
import concourse.bass as bass
import concourse.mybir as mybir
from concourse.bass_utils import run_bass_kernel_spmd

F32 = mybir.dt.float32
BF16 = mybir.dt.bfloat16
AF = mybir.ActivationFunctionType
ALU = mybir.AluOpType

D = 1024
PW = 3072
PLE = 256
EPS = 1e-6
HALO = 16
NCORES = 8


class Sched:
    ENGS = ("pe", "act", "dve", "pool", "sp")

    def __init__(self):
        self.ops = {e: [] for e in self.ENGS}
        self.last_writer = {}
        self.readers = {}
        self.prev_access = {}
        self.dma_count = {}
        self.total_wait = set()

    def op(self, eng, fn, reads=(), writes=(), dma=None):
        idx = len(self.ops[eng])
        deps = {}

        def add(d, raw):
            if d is None:
                return
            deps[d] = deps.get(d, False) or raw

        for k in reads:
            if isinstance(k, str) and k.startswith("PS:"):
                pa = self.prev_access.get(k)
                if pa is not None:
                    add((pa[0], pa[1]), pa[2])
            else:
                add(self.last_writer.get(k), True)
        for k in writes:
            if isinstance(k, str) and k.startswith("PS:"):
                pa = self.prev_access.get(k)
                if pa is not None:
                    add((pa[0], pa[1]), False)
            else:
                add(self.last_writer.get(k), False)
                for r in self.readers.get(k, ()):
                    add(r, False)
        rec = dict(fn=fn, deps=deps, dma=dma, signal=False, dma_val=None)
        if dma is not None:
            self.dma_count[dma] = self.dma_count.get(dma, 0) + 16
            rec["dma_val"] = self.dma_count[dma]
        self.ops[eng].append(rec)
        for k in reads:
            if isinstance(k, str) and k.startswith("PS:"):
                self.prev_access[k] = (eng, idx, False)
            else:
                self.readers.setdefault(k, []).append((eng, idx))
        for k in writes:
            if isinstance(k, str) and k.startswith("PS:"):
                self.prev_access[k] = (eng, idx, True)
            else:
                self.last_writer[k] = (eng, idx)
                self.readers[k] = []
        return (eng, idx)

    def finalize(self):
        for eng in self.ENGS:
            for rec in self.ops[eng]:
                keep = []
                for (de, di), raw in rec["deps"].items():
                    drec = self.ops[de][di]
                    if de == eng and drec["dma"] is None:
                        if eng == "pe":
                            continue
                    keep.append((de, di))
                    if drec["dma"] is None:
                        drec["signal"] = True
                rec["keep"] = keep
        for eng in self.ENGS:
            c = 0
            for rec in self.ops[eng]:
                if rec["dma"] is None and rec["signal"]:
                    c += 1
                    rec["sig_val"] = c

    def emit(self, eng, engine, sems, dma_sems):
        known = {}
        for rec in self.ops[eng]:
            waits = {}
            for (de, di) in rec["keep"]:
                drec = self.ops[de][di]
                if drec["dma"] is not None:
                    key = ("dma", drec["dma"])
                    val = drec["dma_val"]
                    if drec["dma"] in self.total_wait:
                        val = self.dma_count[drec["dma"]]
                else:
                    key = ("eng", de)
                    val = drec["sig_val"]
                if val > waits.get(key, 0):
                    waits[key] = val
            for key, val in waits.items():
                if known.get(key, 0) >= val:
                    continue
                known[key] = val
                sem = dma_sems[key[1]] if key[0] == "dma" else sems[key[1]]
                engine.wait_ge(sem, val)
            ins = rec["fn"](engine)
            if rec["dma"] is not None:
                ins.then_inc(dma_sems[rec["dma"]], 16)
            elif rec["signal"]:
                ins.then_inc(sems[eng], 1)


def build(NPS):
    NT = NPS + 1
    NPTOK = NPS * 512
    nc = bass.Bass("TRN2", target_bir_lowering=False)

    def din(name, shape):
        return nc.dram_tensor(name, shape, F32, kind="ExternalInput").ap()

    def dout(name, shape):
        return nc.dram_tensor(name, shape, F32, kind="ExternalOutput").ap()

    xp = din("xp", [HALO + NPTOK, D])
    pp = din("pp", [NPTOK, PLE])
    xs = din("xs", [128, D])
    ps_in = din("ps", [128, PLE])
    cconvT = din("cconvT", [128, 4, 4, 2])
    cpoolT = din("cpoolT", [128, 4, 4, 15])
    w_in = din("w_in", [24, 128, D])
    w_out = din("w_out", [D, D])
    w_gate = din("w_gate", [D, D])
    w_ple = din("w_ple", [PLE, D])
    pool_w = din("pool_w", [4, 128, 128])
    prm_in = din("prm", [128, 36])
    gfin_in = din("gfin", [128, D])
    icnt_in = din("icnt", [128, 4, 16])
    ident_in = nc.dram_tensor("ident", [128, 128], BF16, kind="ExternalInput").ap()

    yp = dout("yp", [NPTOK, D])
    ys = dout("ys", [128, D])
    sconv_p = dout("sconv_p", [128, 4, 2])
    spool_p = dout("spool_p", [128, 4, 15])
    sconv_s = dout("sconv_s", [128, 4, 4, 2])
    spool_s = dout("spool_s", [128, 4, 4, 15])

    S = Sched()
    es = ExitStack()
    with es:
        def sb(name, shape, dt):
            return es.enter_context(nc.sbuf_tensor(name, shape, dt))

        def psum(name, shape, dt):
            return es.enter_context(nc.psum_tensor(name, shape, dt))

        NXB = 10
        NPB = 1
        xh = sb("xh", [128, NXB, D], F32)
        pin = sb("pin", [128, NPB, PLE], F32)
        winb = sb("winb", [128, 8, PW], BF16)
        woutb = sb("woutb", [128, 8, D], BF16)
        wgb = sb("wgb", [128, 8, D], BF16)
        wpb = sb("wpb", [128, 2, D], BF16)
        pwb = sb("pwb", [128, 4, 128], BF16)
        prm = sb("prm_sb", [128, 36], F32)
        gfin = sb("gfin_sb", [128, D], F32)
        icnt = sb("icnt_sb", [128, 4, 16], F32)
        idb = sb("idb", [128, 128], BF16)
        nh = sb("nh", [128, 1], F32)
        NHN = 3
        NPT = 3
        hn = [sb(f"hn{i}", [128, D], BF16) for i in range(NHN)]
        pb = [sb(f"pb{i}", [128, PLE], BF16) for i in range(1)]
        hnT = [sb(f"hnT{i}", [128, 8, 512], BF16) for i in range(2)]
        hnTh = sb("hnTh", [128, 8, HALO], BF16)
        yT = [sb(f"yT{i}", [128, 8, 512], BF16) for i in range(2)]
        hn2T = [sb(f"hn2T{i}", [128, 8, 128], BF16) for i in range(2)]
        pT = [sb(f"pT{i}", [128, 2, 128], BF16) for i in range(NPT)]
        th = [sb(f"th{i}", [128, D], F32) for i in range(1)]
        xhalo = th[0][0:HALO, :]
        UP = [sb(f"UP{c}", [128, 1, 2 + 512], F32) for c in range(4)]
        VP = [sb(f"VP{g}", [128, 1, 16 + 512], F32) for g in range(4)]
        US = [UP[c][:, 0, 0:4 * 34].rearrange("p (s t) -> p s t", s=4) for c in range(4)]
        VS = [VP[g][:, 0, 0:4 * 48].rearrange("p (s t) -> p s t", s=4) for g in range(4)]
        NPAR = 1
        tA2 = [sb(f"tA2_{i}", [128, 512], F32) for i in range(NPAR)]
        tA3 = [sb(f"tA3_{i}", [128, 512], F32) for i in range(NPAR)]
        tB1 = [sb(f"tB1_{i}", [128, 512], F32) for i in range(NPAR)]
        sA = [sb(f"sA_{i}", [128, 16 + 512], F32) for i in range(NPAR)]
        sB = [sb(f"sB_{i}", [128, 16 + 512], F32) for i in range(NPAR)]
        tB3 = [sb(f"tB3_{i}", [128, 512], BF16) for i in range(NPAR)]
        t16 = sb("t16", [128, 16], F32)
        t16h = sb("t16h", [128, 16], F32)
        icntp = sb("icntp", [128, 16], F32)
        NST = 3 * (NT * 4 + 2) + 8
        st = sb("stat", [128, NST], F32)

        TR = psum("TR", [128, 1024], BF16)
        NF, NTB = 4, 3
        FB = [psum(f"FB{i}", [128, 512], F32) for i in range(NF)]
        TB = [psum(f"TB{i}", [128, 512], F32) for i in range(NTB)]

        sems = {e: es.enter_context(nc.semaphore("s_" + e)) for e in Sched.ENGS}
        dma_names = ([f"ldx{i}" for i in range(NXB)] + [f"ldp{i}" for i in range(NPB)]
                     + [f"sty{i}" for i in range(NXB)] + [f"stg{i}" for i in range(4)]
                     + ["misc", "halo", "statep", "states", "cache", "gfin", "wo"])
        dma_sems = {n: es.enter_context(nc.semaphore("d_" + n)) for n in dma_names}
        S.total_wait.add("misc")
        S.total_wait.add("cache")
        S.total_wait.add("wo")
        S.total_wait.add("statep")
        S.total_wait.add("states")

        ctr = dict(stat=0, f=0, t=0, hn=0, par=0)

        def stat_alloc(n):
            c = ctr["stat"]
            ctr["stat"] += n
            assert ctr["stat"] <= NST
            return c

        def falloc():
            b = ctr["f"] % NF
            ctr["f"] += 1
            return b

        def talloc():
            b = ctr["t"] % NTB
            ctr["t"] += 1
            return b

        class Ring:
            def __init__(self, n, name):
                self.free = list(range(n))
                self.name = name

            def alloc(self):
                assert self.free, f"ring {self.name} exhausted"
                return self.free.pop(0)

            def release(self, s):
                assert s not in self.free
                self.free.append(s)

        hn_ring = Ring(NHN, "hn")
        pT_ring = Ring(NPT, "pT")
        hn2T_ring = Ring(2, "hn2T")

        def hnalloc():
            return hn_ring.alloc()

        S.op("sp", lambda e: e.dma_start(out=prm[:], in_=prm_in), writes=["prm"], dma="misc")
        S.op("sp", lambda e: e.dma_start(out=idb[:], in_=ident_in), writes=["idb"], dma="misc")
        S.op("sp", lambda e: e.dma_start(out=xhalo, in_=xp[0:HALO, :]), writes=[("th", 0, 0), ("th", 0, 1)], dma="halo")

        slot_of = {}
        ldstate = dict(next=0, stores=0)

        def xslot(i, j):
            return slot_of[(i, j)]

        def nsub(i):
            return 4 if i < NPS else 1

        olist = [(i, j) for i in range(NT) for j in range(nsub(i))]
        oindex = {ij: q for q, ij in enumerate(olist)}

        def emit_load(q):
            i, j = olist[q]
            slot = q % NXB
            slot_of[(i, j)] = slot
            if i < NPS:
                r0 = (i * 4 + j) * 128
                xsrc = xp[HALO + r0:HALO + r0 + 128, :]
            else:
                xsrc = xs
            S.op("sp", lambda e: e.dma_start(out=xh[:, slot, :], in_=xsrc), writes=[("xh", slot)], dma=f"ldx{slot}")

        def emit_pload(q):
            if q >= len(olist):
                return
            i, j = olist[q]
            ps_ = q % NPB
            if i < NPS:
                r0 = (i * 4 + j) * 128
                psrc = pp[r0:r0 + 128, :]
            else:
                psrc = ps_in
            S.op("sp", lambda e: e.dma_start(out=pin[:, ps_, :], in_=psrc), writes=[("pin", ps_)], dma=f"ldp{ps_}")

        def try_loads(limit=None):
            while ldstate["next"] < len(olist) and ldstate["next"] < ldstate["stores"] + NXB:
                if limit is not None and ldstate["next"] >= limit:
                    break
                emit_load(ldstate["next"])
                ldstate["next"] += 1

        try_loads(limit=4)
        S.op("sp", lambda e: e.dma_start(out=icnt[:], in_=icnt_in), writes=["icnt"], dma="misc")
        emit_pload(0)

        def emit_cache_loads():
            for c in range(4):
                S.op("sp", lambda e, c=c: e.dma_start(out=US[c][:, :, 0:2], in_=cconvT[:, c, :, :]),
                     writes=[("U", c)], dma="cache")
                S.op("sp", lambda e, c=c: e.dma_start(out=VS[c][:, :, 1:16], in_=cpoolT[:, c, :, :]),
                     writes=[("V", c)], dma="cache")
        S.op("pool", lambda e: e.memset(nh[:], -0.5), writes=["nh"])

        yT1f = yT[1][:].bitcast(F32).rearrange("p k n -> p (k n)")
        stg_views = [xh[:, NXB - 1, :], th[0][:, :], yT1f[:, 0:1024], yT1f[:, 1024:2048]]
        stg_keys = [[("xh", NXB - 1)], [("th", 0, 0), ("th", 0, 1)],
                    [(("yT", 1), c) for c in range(4)], [(("yT", 1), c) for c in range(4, 8)]]
        stg_ctr = [0]

        def stage(src_ap, shape3, consume):
            i = stg_ctr[0] % 4
            stg_ctr[0] += 1
            v = stg_views[i]
            if shape3 is not None:
                ncol = shape3[0] * shape3[1]
                v = v[:, 0:ncol].rearrange("p (a n) -> p a n", a=shape3[0])
            S.op("sp", lambda e: e.dma_start(out=v, in_=src_ap), writes=stg_keys[i], dma=f"stg{i}")
            consume(v, stg_keys[i])

        chunk_groups = []
        for c in range(4):
            chunk_groups.append([16 + c, 20 + c])
            chunk_groups.append([c, 8 + c, 12 + c, 4 + c])
        gmix_b = prm[:, 0:8].unsqueeze(2).to_broadcast([128, 8, 128])
        win_ctr = [0]

        def emit_win_group(gidx):
            if gidx >= len(chunk_groups):
                return
            for cc in chunk_groups[gidx]:
                eng = "dve"
                win_ctr[0] += 1

                def consume(v, keys, cc=cc, eng=eng):
                    S.op(eng, lambda e: e.tensor_tensor(out=winb[:, :, cc * 128:(cc + 1) * 128], in0=v, in1=gmix_b,
                                                        op=ALU.mult),
                         reads=keys + ["prm"], writes=[("winb", cc)])
                stage(w_in[cc].rearrange("p (k n) -> p k n", k=8), (8, 128), consume)

        def winb_key(k, c):
            return ("winb", c)

        def emit_wout_stage(k):
            S.op("pool", lambda e: e.dma_start(out=woutb[:, k, :], in_=w_out[k * 128:(k + 1) * 128, :]),
                 writes=[("woutb", k)], dma="wo")

        def emit_wgate_stage(k):
            def consume(v, keys):
                S.op("act", lambda e: e.activation(out=wgb[:, k, :], in_=v, func=AF.Copy, scale=prm[:, 8 + k:9 + k]),
                     reads=keys + ["prm"], writes=[("wgb", k)])
            stage(w_gate[k * 128:(k + 1) * 128, :], None, consume)

        def emit_wple_stage(k):
            def consume(v, keys):
                S.op("act", lambda e: e.activation(out=wpb[:, k, :], in_=v, func=AF.Copy, scale=0.5),
                     reads=keys, writes=[("wpb", k)])
            stage(w_ple[k * 128:(k + 1) * 128, :], None, consume)

        def emit_poolw_stage():
            def consume(v, keys):
                S.op("act", lambda e: e.activation(out=pwb[:], in_=v, func=AF.Copy), reads=keys, writes=["pwb"])
            stage(pool_w.rearrange("g c d -> c g d"), (4, 128), consume)

        chains = []

        def add_chain(stages):
            ch = list(stages)
            ch.pop(0)()
            if ch:
                chains.append(ch)
            return ch

        def tick():
            for ch in list(chains):
                ch.pop(0)()
                if not ch:
                    chains.remove(ch)

        def finish(ch):
            while ch:
                ch.pop(0)()
            if ch in chains:
                chains.remove(ch)

        def emit_norm(x_ap, ntok, xkeys, scale_eng="act"):
            c = stat_alloc(1)
            h = hnalloc()
            ss = st[0:ntok, c:c + 1]
            ch = add_chain([
                lambda: S.op("act", lambda e: e.activation(out=hn[h][0:ntok, :], in_=x_ap, func=AF.Square, accum_out=ss),
                             reads=xkeys, writes=[("st", c), ("hn", h)]),
                lambda: (S.op("dve", lambda e: e.tensor_scalar(out=ss, in0=ss, scalar1=1.0 / D, scalar2=EPS, op0=ALU.mult,
                                                               op1=ALU.add), reads=[("st", c)], writes=[("st", c)]),
                         S.op("pool", lambda e: e.tensor_tensor(out=ss, in0=ss, in1=nh[0:ntok, :], op=ALU.pow),
                              reads=[("st", c), "nh"], writes=[("st", c)])),
                lambda: (S.op("act", lambda e: e.activation(out=hn[h][0:ntok, :], in_=x_ap, func=AF.Copy, scale=ss),
                              reads=xkeys + [("st", c)], writes=[("hn", h)]) if scale_eng == "act" else
                         S.op("dve", lambda e: e.tensor_scalar(out=hn[h][0:ntok, :], in0=x_ap, scalar1=ss, scalar2=None,
                                                               op0=ALU.mult),
                              reads=xkeys + [("st", c)], writes=[("hn", h)])),
            ])
            return h, ch

        def emit_transposes(src_tile, nch, ntok, src_keys, dst_ap, dst_keys):
            for k in range(nch):
                S.op("pe", lambda e, k=k: e.transpose(out=TR[:, k * 128:k * 128 + ntok],
                                                     in_=src_tile[0:ntok, k * 128:(k + 1) * 128],
                                                     identity=idb[0:ntok, 0:ntok]),
                     reads=src_keys + ["idb"], writes=["PS:TR"])
            src = TR[:, 0:nch * 128].rearrange("p (k t) -> p k t", k=nch)[:, :, 0:ntok]
            S.op("dve", lambda e: e.tensor_copy(out=dst_ap, in_=src), reads=["PS:TR"], writes=dst_keys)

        def proj_chunk(c, src_ap, src_keys, n):
            b = falloc()
            for k in range(8):
                S.op("pe", lambda e, k=k: e.matmul(FB[b][:, 0:n], lhsT=winb[:, k, c * 128:(c + 1) * 128],
                                                   rhs=src_ap[:, k, :], start=(k == 0), stop=(k == 7)),
                     reads=src_keys + [winb_key(k, c)], writes=[f"PS:F{b}"])
            tick()
            return b

        def cw(jj, c):
            col = 16 + jj * 4 + c
            return prm[:, col:col + 1]

        def cbias(c):
            return prm[:, 28 + c:29 + c]

        def pscale(g):
            return prm[:, 32 + g:33 + g]

        def mixer_A(c, tile):
            S_, T_, n = tile["S"], tile["T"], tile["n"]
            U = tile["U"][c]
            ukey = ("U", c)
            src, skeys = tile["hnT"], tile["hnT_keys"]
            par = 0

            def v3(ap):
                return ap.rearrange("p (s t) -> p s t", s=S_)

            bh = proj_chunk(c, src, skeys, n)
            S.op("act", lambda e: e.activation(out=U[:, :, 2:2 + T_], in_=v3(FB[bh][:, 0:n]), func=AF.Copy),
                 reads=[f"PS:F{bh}"], writes=[ukey])
            bc = proj_chunk(8 + c, src, skeys, n)
            S.op("dve", lambda e: e.tensor_tensor(out=U[:, :, 2:2 + T_], in0=v3(FB[bc][:, 0:n]),
                                                  in1=U[:, :, 2:2 + T_], op=ALU.mult),
                 reads=[f"PS:F{bc}", ukey], writes=[ukey])
            bz = proj_chunk(12 + c, src, skeys, n)
            S.op("act", lambda e: e.activation(out=tA2[par][:, 0:n], in_=FB[bz][:, 0:n], func=AF.Silu),
                 reads=[f"PS:F{bz}"], writes=[("tA2", par)])
            t3 = v3(tA3[par][:, 0:n])
            S.op("act", lambda e: e.activation(out=t3, in_=U[:, :, 0:T_], func=AF.Identity, scale=cw(0, c), bias=cbias(c)),
                 reads=[ukey, "prm"], writes=[("tA3", par)])
            S.op("dve", lambda e: e.scalar_tensor_tensor(out=t3, in0=U[:, :, 1:1 + T_], scalar=cw(1, c), in1=t3,
                                                         op0=ALU.mult, op1=ALU.add),
                 reads=[ukey, ("tA3", par), "prm"], writes=[("tA3", par)])
            S.op("dve", lambda e: e.scalar_tensor_tensor(out=t3, in0=U[:, :, 2:2 + T_], scalar=cw(2, c), in1=t3,
                                                         op0=ALU.mult, op1=ALU.add),
                 reads=[ukey, ("tA3", par), "prm"], writes=[("tA3", par)])
            bb = proj_chunk(4 + c, src, skeys, n)
            S.op("dve", lambda e: e.tensor_tensor(out=tA3[par][:, 0:n], in0=FB[bb][:, 0:n], in1=tA3[par][:, 0:n],
                                                  op=ALU.mult),
                 reads=[f"PS:F{bb}", ("tA3", par)], writes=[("tA3", par)])
            S.op("pool", lambda e: e.tensor_tensor(out=tile["yT"][:, c, :], in0=tA3[par][:, 0:n], in1=tA2[par][:, 0:n],
                                                   op=ALU.mult),
                 reads=[("tA3", par), ("tA2", par)], writes=[(tile["yT_key"], c)])
            if tile["kind"] == "p":
                if tile["last"]:
                    S.op("sp", lambda e: e.dma_start(out=sconv_p[:, c, :], in_=U[:, 0, T_:T_ + 2]),
                         reads=[ukey], dma="statep")
                else:
                    S.op("pool", lambda e: e.tensor_copy(out=U[:, :, 0:2], in_=U[:, :, T_:T_ + 2]),
                         reads=[ukey], writes=[ukey])
            else:
                S.op("sp", lambda e: e.dma_start(out=sconv_s[:, c, :, :], in_=U[:, :, T_:T_ + 2]),
                     reads=[ukey], dma="states")

        def mixer_B_front(g, tile):
            S_, T_, n = tile["S"], tile["T"], tile["n"]
            V = tile["V"][g]
            vkey = ("V", g)
            src, skeys = tile["hnT"], tile["hnT_keys"]
            par = 0
            W = 2 ** (g + 1)

            def v3(ap):
                return ap.rearrange("p (s t) -> p s t", s=S_)

            def sv(buf):
                return buf[:, 0:S_ * (16 + T_)].rearrange("p (s t) -> p s t", s=S_)

            bv = proj_chunk(16 + g, src, skeys, n)
            S.op("act", lambda e: e.activation(out=V[:, :, 16:16 + T_], in_=v3(FB[bv][:, 0:n]), func=AF.Copy),
                 reads=[f"PS:F{bv}"], writes=[vkey])
            bz = proj_chunk(20 + g, src, skeys, n)
            S.op("act", lambda e: e.activation(out=tB1[par][:, 0:n], in_=FB[bz][:, 0:n], func=AF.Silu),
                 reads=[f"PS:F{bz}"], writes=[("tB1", par)])
            a3, b3 = sv(sA[par]), sv(sB[par])
            aeng = "dve" if g < 3 else "pool"
            stages = []
            lo = -(W - 2)
            stages.append(lambda lo=lo: S.op(aeng, lambda e: e.tensor_tensor(
                out=a3[:, :, 16 + lo:16 + T_], in0=V[:, :, 16 + lo:16 + T_], in1=V[:, :, 15 + lo:15 + T_], op=ALU.add),
                reads=[vkey], writes=[("sA", par)]))
            fin, fkey = a3, ("sA", par)
            if W >= 4:
                lo = -(W - 4)
                stages.append(lambda lo=lo: S.op(aeng, lambda e: e.tensor_tensor(
                    out=b3[:, :, 16 + lo:16 + T_], in0=a3[:, :, 16 + lo:16 + T_], in1=a3[:, :, 14 + lo:14 + T_], op=ALU.add),
                    reads=[("sA", par)], writes=[("sB", par)]))
                fin, fkey = b3, ("sB", par)
            if W >= 8:
                lo = -(W - 8)
                stages.append(lambda lo=lo: S.op(aeng, lambda e: e.tensor_tensor(
                    out=a3[:, :, 16 + lo:16 + T_], in0=b3[:, :, 16 + lo:16 + T_], in1=b3[:, :, 12 + lo:12 + T_], op=ALU.add),
                    reads=[("sB", par)], writes=[("sA", par)]))
                fin, fkey = a3, ("sA", par)
            if W >= 16:
                stages.append(lambda: S.op(aeng, lambda e: e.tensor_tensor(
                    out=b3[:, :, 16:16 + T_], in0=a3[:, :, 16:16 + T_], in1=a3[:, :, 8:8 + T_], op=ALU.add),
                    reads=[("sA", par)], writes=[("sB", par)]))
                fin, fkey = b3, ("sB", par)

            def s_pooled(fin=fin, fkey=fkey):
                if aeng == "dve":
                    S.op("dve", lambda e: e.scalar_tensor_tensor(out=v3(tB3[par][:, 0:n]), in0=fin[:, :, 16:16 + T_],
                                                                 scalar=1.0 / W, in1=V[:, :, 16:16 + T_],
                                                                 op0=ALU.mult, op1=ALU.subtract),
                         reads=[fkey, vkey], writes=[("tB3", par)])
                else:
                    S.op("pool", lambda e: e.tensor_scalar(out=fin[:, :, 16:16 + T_], in0=fin[:, :, 16:16 + T_],
                                                           scalar1=1.0 / W, scalar2=0.0, op0=ALU.mult, op1=ALU.add),
                         reads=[fkey], writes=[fkey])
                    S.op("pool", lambda e: e.tensor_tensor(out=v3(tB3[par][:, 0:n]), in0=fin[:, :, 16:16 + T_],
                                                           in1=V[:, :, 16:16 + T_], op=ALU.subtract),
                         reads=[fkey, vkey], writes=[("tB3", par)])
                if tile["first"]:
                    S.op(aeng, lambda e: e.tensor_tensor(out=t16[:], in0=fin[:, 0, 16:32],
                                                         in1=(icnt[:, g, :] if aeng == "dve" else icntp[:, :]), op=ALU.mult),
                         reads=[fkey, "icnt"], writes=["t16"])
                    S.op(aeng, lambda e: e.tensor_tensor(out=tB3[par][:, 0:16], in0=t16[:], in1=V[:, 0, 16:32],
                                                         op=ALU.subtract),
                         reads=["t16", vkey], writes=[("tB3", par)])
            adds = stages
            stages = []
            for a_ in range(0, len(adds), 2):
                grp = adds[a_:a_ + 2]
                stages.append(lambda grp=grp: [f_() for f_ in grp])
            stages.append(s_pooled)
            def s_hist():
                if tile["kind"] == "p":
                    if tile["last"]:
                        S.op("sp", lambda e: e.dma_start(out=spool_p[:, g, :], in_=V[:, 0, T_ + 1:T_ + 16]),
                             reads=[vkey], dma="statep")
                    else:
                        S.op("pool", lambda e: e.tensor_copy(out=V[:, :, 0:16], in_=V[:, :, T_:T_ + 16]),
                             reads=[vkey], writes=[vkey])
                else:
                    S.op("sp", lambda e: e.dma_start(out=spool_s[:, g, :, :], in_=V[:, :, T_ + 1:T_ + 16]),
                         reads=[vkey], dma="states")
            stages.append(s_hist)
            bch = add_chain(stages)
            return par, bch

        def mixer_B_back(g, tile, par_ch):
            par, bch = par_ch
            finish(bch)
            n = tile["n"]
            bm = falloc()
            S.op("pe", lambda e: e.matmul(FB[bm][:, 0:n], lhsT=pwb[:, g, :], rhs=tB3[par][:, 0:n], start=True, stop=True),
                 reads=[("tB3", par), "pwb"], writes=[f"PS:F{bm}"])
            tick()
            S.op("dve", lambda e: e.scalar_tensor_tensor(out=tile["yT"][:, 4 + g, :], in0=FB[bm][:, 0:n], scalar=pscale(g),
                                                         in1=tB1[par][:, 0:n], op0=ALU.mult, op1=ALU.mult),
                 reads=[f"PS:F{bm}", ("tB1", par), "prm"], writes=[(tile["yT_key"], 4 + g)])

        def emit_halo_front():
            h, ch = emit_norm(xhalo, HALO, [("th", 0, 0), ("th", 0, 1)])
            finish(ch)
            emit_transposes(hn[h], 8, HALO, [("hn", h)], hnTh[:], ["hnTh"])
            hn_ring.release(h)

        def emit_halo_A(c):
            bh = proj_chunk(c, hnTh, ["hnTh"], HALO)
            S.op("act", lambda e: e.activation(out=t16h[:, 0:HALO], in_=FB[bh][:, 0:HALO], func=AF.Copy),
                 reads=[f"PS:F{bh}"], writes=["t16h"])
            bc = proj_chunk(8 + c, hnTh, ["hnTh"], HALO)
            S.op("dve", lambda e: e.tensor_tensor(out=UP[c][:, 0, 0:2], in0=FB[bc][:, HALO - 2:HALO],
                                                  in1=t16h[:, HALO - 2:HALO], op=ALU.mult),
                 reads=[f"PS:F{bc}", "t16h"], writes=[("U", c)])

        def emit_halo_B(g):
            bv = proj_chunk(16 + g, hnTh, ["hnTh"], HALO)
            S.op("act", lambda e: e.activation(out=VP[g][:, 0, 0:16], in_=FB[bv][:, 0:HALO], func=AF.Copy),
                 reads=[f"PS:F{bv}"], writes=[("V", g)])

        def tile_desc(i):
            par = i % 2
            if i < NPS:
                return dict(kind="p", S=1, T=512, n=512, U=UP, V=VP, hnT=hnT[par][:, :, :], hnT_keys=[("hnT", par)],
                            yT=yT[par][:, :, :], yT_key=("yT", par), first=(i == 0), last=(i == NPS - 1), idx=i)
            return dict(kind="s", S=4, T=32, n=128, U=US, V=VS, hnT=hnT[par][:, :, 0:128], hnT_keys=[("hnT", par)],
                        yT=yT[par][:, :, 0:128], yT_key=("yT", par), first=False, last=False, idx=i)

        nstate = {}

        def emit_N_norm(i, j, scale_eng="act"):
            while (i, j) not in slot_of and chains:
                tick()
            slot = xslot(i, j)
            nstate[(i, j)] = emit_norm(xh[:, slot, :], 128, [("xh", slot)], scale_eng=scale_eng)

        def emit_N_tr(i, j):
            par = i % 2
            h, ch = nstate[(i, j)]
            finish(ch)
            emit_transposes(hn[h], 8, 128, [("hn", h)], hnT[par][:, :, j * 128:(j + 1) * 128], [("hnT", par)])
            hn_ring.release(h)

        def emit_N(i, j):
            emit_N_norm(i, j)
            emit_N_tr(i, j)

        ostate = {}

        def emit_pcast(q):
            if q >= len(olist):
                return
            S.op("act", lambda e: e.activation(out=pb[0][:], in_=pin[:, q % NPB, :], func=AF.Copy),
                 reads=[("pin", q % NPB)], writes=[("pb", 0)])

        def emit_O_wout(i, j):
            slot = xslot(i, j)
            par = i % 2
            b0, b1 = talloc(), talloc()
            ykeys = [(("yT", par), c) for c in range(8)]
            for k in range(8):
                for hf, b in ((0, b0), (1, b1)):
                    S.op("pe", lambda e, k=k, hf=hf, b=b: e.matmul(TB[b][:, :], lhsT=yT[par][:, k, j * 128:(j + 1) * 128],
                                                                   rhs=woutb[:, k, hf * 512:(hf + 1) * 512],
                                                                   start=(k == 0), stop=(k == 7)),
                         reads=ykeys + [("woutb", k)], writes=[f"PS:T{b}"])
            tick()
            q_o = oindex[(i, j)]
            pslot = pT_ring.alloc()
            pbslot = 0
            emit_pcast(q_o)
            emit_transposes(pb[pbslot], 2, 128, [("pb", pbslot)], pT[pslot][:], [("pT", pslot)])
            emit_pload(q_o + 1)
            for hf, b in ((0, b0), (1, b1)):
                S.op("dve", lambda e, hf=hf, b=b: e.tensor_tensor(out=xh[:, slot, hf * 512:(hf + 1) * 512], in0=TB[b][:, :],
                                                                  in1=xh[:, slot, hf * 512:(hf + 1) * 512], op=ALU.add),
                     reads=[f"PS:T{b}", ("xh", slot)], writes=[("xh", slot)])
            h, ch = emit_norm(xh[:, slot, :], 128, [("xh", slot)])
            ostate[(i, j)] = dict(h=h, pslot=pslot, ch=ch)

        def emit_O_tr2(i, j):
            o = ostate[(i, j)]
            finish(o["ch"])
            q = hn2T_ring.alloc()
            emit_transposes(hn[o["h"]], 8, 128, [("hn", o["h"])], hn2T[q][:], [("hn2T", q)])
            hn_ring.release(o["h"])
            o["q"] = q

        def emit_O_gate(i, j):
            o = ostate[(i, j)]
            slot = xslot(i, j)
            q, pslot = o["q"], o["pslot"]
            g0, g1 = talloc(), talloc()
            for k in range(8):
                for hf, b in ((0, g0), (1, g1)):
                    S.op("pe", lambda e, k=k, hf=hf, b=b: e.matmul(TB[b][:, :], lhsT=hn2T[q][:, k, :],
                                                                   rhs=wgb[:, k, hf * 512:(hf + 1) * 512],
                                                                   start=(k == 0), stop=(k == 7)),
                         reads=[("hn2T", q), ("wgb", k)], writes=[f"PS:T{b}"])
            tick()
            tq = 0
            for hf, b in ((0, g0), (1, g1)):
                S.op("act", lambda e, hf=hf, b=b: e.activation(out=th[tq][:, hf * 512:(hf + 1) * 512], in_=TB[b][:, :],
                                                               func=AF.Tanh, scale=0.5),
                     reads=[f"PS:T{b}"], writes=[("th", tq, hf)])
            p0, p1 = talloc(), talloc()
            for k in range(2):
                for hf, b in ((0, p0), (1, p1)):
                    S.op("pe", lambda e, k=k, hf=hf, b=b: e.matmul(TB[b][:, :], lhsT=pT[pslot][:, k, :],
                                                                   rhs=wpb[:, k, hf * 512:(hf + 1) * 512],
                                                                   start=(k == 0), stop=(k == 1)),
                         reads=[("pT", pslot), ("wpb", k)], writes=[f"PS:T{b}"])
            tick()
            for hf, b in ((0, p0), (1, p1)):
                S.op("dve", lambda e, hf=hf, b=b: e.scalar_tensor_tensor(out=th[tq][:, hf * 512:(hf + 1) * 512],
                                                                         in0=th[tq][:, hf * 512:(hf + 1) * 512], scalar=1.0,
                                                                         in1=TB[b][:, :], op0=ALU.add, op1=ALU.mult),
                     reads=[f"PS:T{b}", ("th", tq, hf)], writes=[("th", tq, hf)])
            hn2T_ring.release(q)
            pT_ring.release(pslot)

        def emit_O_fin(i, j):
            slot = xslot(i, j)
            x_ap = xh[:, slot, :]
            xkeys = [("xh", slot)]
            c = stat_alloc(1)
            ss = st[:, c:c + 1]
            if i < NPS:
                r0 = (i * 4 + j) * 128
                dst = yp[r0:r0 + 128, :]
            else:
                dst = ys

            def s00():
                S.op("pool", lambda e: e.tensor_tensor(out=x_ap, in0=x_ap, in1=th[0][:], op=ALU.add),
                     reads=xkeys + [("th", 0, 0), ("th", 0, 1)], writes=xkeys)

            def s0():
                S.op("act", lambda e: e.activation(out=th[0][:, :], in_=x_ap, func=AF.Square, accum_out=ss),
                     reads=xkeys, writes=[("st", c), ("th", 0, 0), ("th", 0, 1)])

            def s1():
                S.op("dve", lambda e: e.tensor_scalar(out=ss, in0=ss, scalar1=1.0 / D, scalar2=EPS, op0=ALU.mult,
                                                      op1=ALU.add), reads=[("st", c)], writes=[("st", c)])

            def s2():
                S.op("pool", lambda e: e.tensor_tensor(out=ss, in0=ss, in1=nh[:, :], op=ALU.pow),
                     reads=[("st", c), "nh"], writes=[("st", c)])

            def s3():
                S.op("dve", lambda e: e.scalar_tensor_tensor(out=x_ap, in0=x_ap, scalar=ss, in1=gfin[:],
                                                             op0=ALU.mult, op1=ALU.mult),
                     reads=xkeys + [("st", c), "gfin"], writes=xkeys)

            def s4():
                S.op("sp", lambda e: e.dma_start(out=dst, in_=x_ap), reads=xkeys, dma=f"sty{slot}")
                ldstate["stores"] += 1
                try_loads()

            def s12():
                s1()
                s2()

            def s34():
                s3()
                s4()

            return add_chain([s00, s0, s12, s34])

        emit_halo_front()
        emit_N_norm(0, 0)
        emit_N_norm(0, 1, scale_eng="dve")
        emit_N_norm(0, 2)
        tick()
        tick()
        emit_N_tr(0, 0)
        emit_N_norm(0, 3, scale_eng="dve")
        tick()
        tick()
        emit_win_group(0)
        emit_N_tr(0, 1)
        emit_N_tr(0, 2)
        emit_win_group(1)
        emit_N_tr(0, 3)
        S.op("pool", lambda e: e.tensor_scalar(out=icntp[:, :], in0=icnt[:, 3, :], scalar1=16.0, scalar2=0.0,
                                               op0=ALU.mult, op1=ALU.add), reads=["icnt"], writes=["icnt"])
        emit_poolw_stage()
        try_loads(limit=6)

        order = [("B", 0), ("A", 0), ("B", 1), ("A", 1), ("B", 2), ("A", 2), ("B", 3), ("A", 3)]
        pendB = None
        for i in range(NT + 1):
            tile = tile_desc(i) if i < NT else None
            if i == NPS:
                while chains:
                    tick()
                emit_cache_loads()
                for gi, (kind, c) in enumerate(order):
                    if kind == "A":
                        mixer_A(c, tile)
                    else:
                        par = mixer_B_front(c, tile)
                    if pendB is not None:
                        mixer_B_back(pendB[0], pendB[2], pendB[1])
                        pendB = None
                    if kind == "B":
                        pendB = (c, par, tile)
                assert pendB is None
                L = [(i - 1, j) for j in range(nsub(i - 1))] + [(i, 0)]
                for s in range(len(L) + 3):
                    deferred = []
                    if 0 <= s - 3 < len(L):
                        emit_O_gate(*L[s - 3])
                        deferred.append(L[s - 3])
                    if 0 <= s - 2 < len(L):
                        emit_O_tr2(*L[s - 2])
                    if s < len(L):
                        emit_O_wout(*L[s])
                    for (di, dj) in deferred:
                        emit_O_fin(di, dj)
                    tick()
                break
            for gi, (kind, c) in enumerate(order):
                deferred = []
                if i == 0:
                    emit_win_group(gi + 2)
                    if kind == "A":
                        emit_halo_A(c)
                    else:
                        emit_halo_B(c)
                if tile is not None:
                    if kind == "A":
                        mixer_A(c, tile)
                    else:
                        par = mixer_B_front(c, tile)
                if pendB is not None:
                    mixer_B_back(pendB[0], pendB[2], pendB[1])
                    pendB = None
                if tile is not None and kind == "B":
                    pendB = (c, par, tile)
                if i == 0:
                    for k in {3: (0, 1, 2, 3), 4: (4, 5, 6, 7)}.get(gi, ()):
                        emit_wout_stage(k)
                    for k in {6: (0, 1, 2, 3), 7: (4, 5, 6, 7)}.get(gi, ()):
                        emit_wgate_stage(k)
                    if gi == 2:
                        try_loads(limit=8)
                    if gi == 7:
                        emit_wple_stage(0)
                        emit_wple_stage(1)
                        S.op("sp", lambda e: e.dma_start(out=gfin[:], in_=gfin_in), writes=["gfin"], dma="gfin")
                    if gi == 7:
                        try_loads()
                if i >= 1:
                    ns = nsub(i - 1)
                    if 0 <= gi - 3 < ns:
                        emit_O_gate(i - 1, gi - 3)
                        deferred.append((i - 1, gi - 3))
                    if 0 <= gi - 2 < ns:
                        emit_O_tr2(i - 1, gi - 2)
                    if gi < ns:
                        emit_O_wout(i - 1, gi)
                if i + 1 < NT:
                    nn = nsub(i + 1)
                    for (jn, g_norm, g_tr) in ((0, 0, 1), (1, 1, 2), (2, 4, 5), (3, 5, 6)):
                        if jn < nn:
                            if gi == g_tr:
                                emit_N_tr(i + 1, jn)
                    for (jn, g_norm, g_tr) in ((0, 0, 1), (1, 1, 2), (2, 4, 5), (3, 5, 6)):
                        if jn < nn:
                            if gi == g_norm:
                                emit_N_norm(i + 1, jn)
                for (di, dj) in deferred:
                    emit_O_fin(di, dj)
                if tile is None:
                    tick()
        while chains:
            tick()

        S.finalize()
        final_waits = [(n, S.dma_count[n]) for n in dma_names if n.startswith("sty") or n.startswith("state")
                       if S.dma_count.get(n, 0) > 0]
        with nc.Block() as block:
            @block.tensor
            def _(e):
                S.emit("pe", e, sems, dma_sems)

            @block.scalar
            def _(e):
                S.emit("act", e, sems, dma_sems)

            @block.vector
            def _(e):
                S.emit("dve", e, sems, dma_sems)

            @block.gpsimd
            def _(e):
                S.emit("pool", e, sems, dma_sems)

            @block.sync
            def _(e):
                S.emit("sp", e, sems, dma_sems)
                for n, v in final_waits:
                    e.wait_ge(dma_sems[n], v)
    return nc


def _prep_inputs(inp, NPS):
    f = lambda a: np.ascontiguousarray(np.asarray(a, dtype=np.float32))
    x_prompt, x_sample = f(inp["x_prompt"]), f(inp["x_sample"])
    p_prompt, p_sample = f(inp["p_prompt"])[0], f(inp["p_sample"])[0]
    cache_conv, cache_pool = f(inp["cache_conv"])[0], f(inp["cache_pool"])[0]
    NPTOK = NPS * 512
    B, SEQ, _ = x_prompt.shape
    cps = NCORES // B
    assert SEQ == cps * NPTOK
    cols = []
    cols += [f(inp["g_mix"])[0].reshape(8, 128)]
    cols += [f(inp["g_ple"])[0].reshape(8, 128)]
    cols += [f(inp["conv_w"])[0].reshape(12, 128)]
    cols += [f(inp["conv_b"])[0].reshape(4, 128)]
    cols += [f(inp["pool_scale"])[0].reshape(4, 128)]
    prm = np.ascontiguousarray(np.concatenate(cols, axis=0).T)
    gfin = np.ascontiguousarray(np.broadcast_to(f(inp["g_final"])[None, :], (128, D)))
    import ml_dtypes
    ident = np.eye(128, dtype=np.float32).astype(ml_dtypes.bfloat16)
    icnt_first = np.zeros((128, 4, 16), np.float32)
    icnt_rest = np.zeros((128, 4, 16), np.float32)
    for g in range(4):
        w = 2 ** (g + 1)
        icnt_first[:, g, :] = 1.0 / np.minimum(np.arange(16) + 1, w)
        icnt_rest[:, g, :] = 1.0 / w
    w_in_r = np.ascontiguousarray(f(inp["w_in"])[0].reshape(8, 128, 24, 128).transpose(2, 1, 0, 3).reshape(24, 128, D))
    shared = dict(w_in=w_in_r, w_out=f(inp["w_out"])[0], w_gate=f(inp["w_ple_gate"])[0],
                  w_ple=f(inp["w_ple"])[0], pool_w=f(inp["pool_w"])[0], prm=prm, gfin=gfin, ident=ident)
    maps = []
    for c in range(NCORES):
        b, ch = divmod(c, cps)
        s0 = ch * NPTOK
        xp = np.zeros((HALO + NPTOK, D), np.float32)
        xp[HALO:] = x_prompt[b, s0:s0 + NPTOK]
        if ch > 0:
            xp[:HALO] = x_prompt[b, s0 - HALO:s0]
        sl = slice(4 * c, 4 * c + 4)
        cc = cache_conv[sl].reshape(4, 2, 4, 128)
        cp = cache_pool[sl].reshape(4, 15, 4, 128)
        m = dict(shared)
        m.update(xp=xp, pp=np.ascontiguousarray(p_prompt[b, s0:s0 + NPTOK]),
                 xs=np.ascontiguousarray(x_sample[sl].reshape(128, D)),
                 ps=np.ascontiguousarray(p_sample[sl].reshape(128, PLE)),
                 cconvT=np.ascontiguousarray(cc.transpose(3, 2, 0, 1)),
                 cpoolT=np.ascontiguousarray(cp.transpose(3, 2, 0, 1)),
                 icnt=icnt_first if ch == 0 else icnt_rest)
        maps.append(m)
    return maps, (B, SEQ, cps)


def _assemble(results, meta, NPS):
    B, SEQ, cps = meta
    NPTOK = NPS * 512
    y_prompt = np.empty((B, SEQ, D), np.float32)
    y_sample = np.empty((32, 32, D), np.float32)
    sc_p = np.empty((1, B, 2, 512), np.float32)
    sp_p = np.empty((1, B, 15, 512), np.float32)
    sc_s = np.empty((1, 32, 2, 512), np.float32)
    sp_s = np.empty((1, 32, 15, 512), np.float32)
    for c, r in enumerate(results):
        b, ch = divmod(c, cps)
        y_prompt[b, ch * NPTOK:(ch + 1) * NPTOK] = r["yp"]
        y_sample[4 * c:4 * c + 4] = r["ys"].reshape(4, 32, D)
        sc_s[0, 4 * c:4 * c + 4] = r["sconv_s"].transpose(2, 3, 1, 0).reshape(4, 2, 512)
        sp_s[0, 4 * c:4 * c + 4] = r["spool_s"].transpose(2, 3, 1, 0).reshape(4, 15, 512)
        if ch == cps - 1:
            sc_p[0, b] = r["sconv_p"].transpose(2, 1, 0).reshape(2, 512)
            sp_p[0, b] = r["spool_p"].transpose(2, 1, 0).reshape(15, 512)
    return (y_prompt, y_sample, sc_p, sp_p, sc_s, sp_s)


_NC_CACHE = {}


def _run(inp, NPS):
    maps, meta = _prep_inputs(inp, NPS)
    if NPS not in _NC_CACHE:
        _NC_CACHE[NPS] = build(NPS)
    res = run_bass_kernel_spmd(_NC_CACHE[NPS], maps, core_ids=list(range(NCORES)))
    return _assemble(res.results, meta, NPS)


def kernel(**inputs):
    return _run(inputs, 8)
```
